# Optimizing a Trainium2 kernel written in Bass

```python
import math
import jax, jax.numpy as jnp
from jax import lax
import numpy as np

D_MODEL = 2048
BATCH = 4
SEQ = 4096
DEPTH = 2

N_MEM = 256
BRANCH_WIDTH = 1024
N_BRANCH = 4
DA_HEADS = 8
DA_QK_DIM = 64
DA_V_DIM = 2 * DA_QK_DIM
SB_HEADS = 8
SB_HEAD_DIM = BRANCH_WIDTH // SB_HEADS
POOL_WINDOWS = (2, 4, 8, 16)
POOL_GROUPS = 4
POOL_GROUP = BRANCH_WIDTH // POOL_GROUPS
MEM_HEADS = 4
MEM_HEAD_DIM = BRANCH_WIDTH // MEM_HEADS
REL_BUCKETS = 32
REL_MAX_DIST = 128
Q_BLOCK = 128
EPS = 1e-6
N_SLICES = 12
IN_COLS = N_SLICES * BRANCH_WIDTH + N_BRANCH * D_MODEL

kernel_name = "hybrid_gated_diff_stickbreak_pool_mem"


def rmsnorm(x, g):
    xf = x.astype(jnp.float32)
    y = xf * lax.rsqrt(jnp.mean(xf * xf, axis=-1, keepdims=True) + EPS)
    return (y * g.astype(jnp.float32)).astype(x.dtype)


def t5_bucket(rel):
    n = jnp.maximum(rel, 0)
    max_exact = REL_BUCKETS // 2
    nf = jnp.maximum(n, 1).astype(jnp.float32)
    large = max_exact + (jnp.log(nf / max_exact) / math.log(REL_MAX_DIST / max_exact)
                         * (REL_BUCKETS - max_exact)).astype(jnp.int32)
    large = jnp.minimum(large, REL_BUCKETS - 1)
    return jnp.where(n < max_exact, n, large)


def to_blocks(t):
    b, s = t.shape[:2]
    t = t.reshape((b, s // Q_BLOCK, Q_BLOCK) + t.shape[2:])
    return jnp.moveaxis(t, 1, 0)


def from_blocks(t):
    t = jnp.moveaxis(t, 0, 1)
    return t.reshape((t.shape[0], -1) + t.shape[3:])


def diff_attention(q, k, v, rel_bias, lam):
    s_len = q.shape[1]
    k_pos = jnp.arange(s_len)
    scale = DA_QK_DIM ** -0.5

    def block(args):
        qb, start = args
        q_pos = start + jnp.arange(Q_BLOCK)
        rel = q_pos[:, None] - k_pos[None, :]
        bias = jnp.transpose(rel_bias[t5_bucket(rel)], (2, 0, 1)).astype(jnp.float32)
        logits = jnp.einsum('bqhcd,bkhcd->bhcqk', qb, k).astype(jnp.float32) * scale
        logits = logits + bias[None, :, None]
        logits = jnp.where(rel >= 0, logits, -jnp.inf)
        p = jax.nn.softmax(logits, axis=-1)
        a = p[:, :, 0] - lam * p[:, :, 1]
        return jnp.einsum('bhqk,bkhd->bqhd', a.astype(v.dtype), v)

    nb = s_len // Q_BLOCK
    out = lax.map(block, (to_blocks(q), jnp.arange(nb) * Q_BLOCK))
    return from_blocks(out)


def stick_breaking(q, k, v):
    s_len = q.shape[1]
    k_pos = jnp.arange(s_len)
    scale = SB_HEAD_DIM ** -0.5

    def block(args):
        qb, start = args
        q_pos = start + jnp.arange(Q_BLOCK)
        mask = k_pos[None, :] < q_pos[:, None]
        z = jnp.einsum('bqhd,bkhd->bhqk', qb, k).astype(jnp.float32) * scale
        log_beta = jax.nn.log_sigmoid(z)
        log_1mb = jnp.where(mask, jax.nn.log_sigmoid(-z), 0.0)
        between = lax.cumsum(log_1mb, axis=3, reverse=True) - log_1mb
        a = jnp.where(mask, jnp.exp(log_beta + between), 0.0)
        return jnp.einsum('bhqk,bkhd->bqhd', a.astype(v.dtype), v)

    nb = s_len // Q_BLOCK
    out = lax.map(block, (to_blocks(q), jnp.arange(nb) * Q_BLOCK))
    return from_blocks(out)


def multiscale_pool(u, w_pool, pool_scale):
    b, s, _ = u.shape
    ug = u.reshape(b, s, POOL_GROUPS, POOL_GROUP).astype(jnp.float32)
    c0 = jnp.concatenate([jnp.zeros((b, 1, POOL_GROUPS, POOL_GROUP), jnp.float32),
                          jnp.cumsum(ug, axis=1)], axis=1)
    t = jnp.arange(s)
    outs = []
    for g, w in enumerate(POOL_WINDOWS):
        cg = c0[:, :, g]
        lo = jnp.concatenate([jnp.zeros((b, w - 1, POOL_GROUP), jnp.float32),
                              cg[:, :s - w + 1]], axis=1)
        count = jnp.minimum(t + 1, w).astype(jnp.float32)[None, :, None]
        outs.append((cg[:, 1:] - lo) / count - ug[:, :, g])
    pooled = jnp.stack(outs, axis=2).astype(u.dtype)
    mixed = jnp.einsum('bsgc,gcd->bsgd', pooled, w_pool)
    return mixed.reshape(b, s, BRANCH_WIDTH) * pool_scale


def memory_attention(q, mk, mv):
    logits = jnp.einsum('bqhd,bmhd->bhqm', q, mk).astype(jnp.float32) * (MEM_HEAD_DIM ** -0.5)
    p = jax.nn.softmax(logits, axis=-1)
    return jnp.einsum('bhqm,bmhd->bqhd', p.astype(mv.dtype), mv)


def hybrid_layer(x, mem, layer_idx, rel_bias, norm_g, w_in, gate_b, lam_q1, lam_k1,
                 lam_q2, lam_k2, da_norm_g, w_pool, pool_scale, mem_norm_g, w_mem_kv,
                 w_branch, w_out):
    b, s, _ = x.shape
    h = rmsnorm(x, norm_g)
    proj = h @ w_in
    (da_q, da_k, da_v, da_z, sb_q, sb_k, sb_v, sb_z,
     pool_u, pool_z, mem_q, mem_z) = jnp.split(proj[..., :N_SLICES * BRANCH_WIDTH], N_SLICES, axis=-1)
    gate_logits = proj[..., N_SLICES * BRANCH_WIDTH:].reshape(b, s, N_BRANCH, D_MODEL) + gate_b

    lam_init = 0.8 - 0.6 * math.exp(-0.3 * layer_idx)
    lam = (jnp.exp(jnp.sum((lam_q1 * lam_k1).astype(jnp.float32)))
           - jnp.exp(jnp.sum((lam_q2 * lam_k2).astype(jnp.float32))) + lam_init)
    o_da = diff_attention(da_q.reshape(b, s, DA_HEADS, 2, DA_QK_DIM),
                          da_k.reshape(b, s, DA_HEADS, 2, DA_QK_DIM),
                          da_v.reshape(b, s, DA_HEADS, DA_V_DIM), rel_bias, lam)
    o_da = rmsnorm(o_da, da_norm_g.reshape(DA_HEADS, DA_V_DIM)) * (1.0 - lam_init)
    o_da = o_da.reshape(b, s, BRANCH_WIDTH)

    o_sb = stick_breaking(sb_q.reshape(b, s, SB_HEADS, SB_HEAD_DIM),
                          sb_k.reshape(b, s, SB_HEADS, SB_HEAD_DIM),
                          sb_v.reshape(b, s, SB_HEADS, SB_HEAD_DIM)).reshape(b, s, BRANCH_WIDTH)

    o_pool = multiscale_pool(pool_u, w_pool, pool_scale)

    mkv = rmsnorm(mem, mem_norm_g) @ w_mem_kv
    mk, mv = jnp.split(mkv, 2, axis=-1)
    m = mem.shape[1]
    o_mem = memory_attention(mem_q.reshape(b, s, MEM_HEADS, MEM_HEAD_DIM),
                             mk.reshape(b, m, MEM_HEADS, MEM_HEAD_DIM),
                             mv.reshape(b, m, MEM_HEADS, MEM_HEAD_DIM)).reshape(b, s, BRANCH_WIDTH)

    branches = (o_da * jax.nn.silu(da_z), o_sb * jax.nn.silu(sb_z),
                o_pool * jax.nn.silu(pool_z), o_mem * jax.nn.silu(mem_z))
    merged = jnp.zeros_like(x)
    for n in range(N_BRANCH):
        merged = merged + jax.nn.sigmoid(gate_logits[:, :, n]) * (branches[n] @ w_branch[n])
    return x + merged @ w_out


def setup_inputs(seed: int = 0) -> dict:
    key = jax.random.key(seed)
    ks = jax.random.split(key, 20)
    f32 = jnp.float32
    W = BRANCH_WIDTH
    nrm = lambda k, shape, s: jax.random.normal(k, shape, f32) * s
    return {
        "x": nrm(ks[0], (BATCH, SEQ, D_MODEL), 1.0),
        "mem": nrm(ks[1], (BATCH, N_MEM, D_MODEL), 1.0),
        "rel_bias": nrm(ks[2], (REL_BUCKETS, DA_HEADS), 0.5),
        "norm_g": 1.0 + nrm(ks[3], (DEPTH, D_MODEL), 0.02),
        "w_in": nrm(ks[4], (DEPTH, D_MODEL, IN_COLS), D_MODEL ** -0.5),
        "gate_b": nrm(ks[5], (DEPTH, N_BRANCH, D_MODEL), 0.02),
        "lam_q1": nrm(ks[6], (DEPTH, DA_QK_DIM), 0.1),
        "lam_k1": nrm(ks[7], (DEPTH, DA_QK_DIM), 0.1),
        "lam_q2": nrm(ks[8], (DEPTH, DA_QK_DIM), 0.1),
        "lam_k2": nrm(ks[9], (DEPTH, DA_QK_DIM), 0.1),
        "da_norm_g": 1.0 + nrm(ks[10], (DEPTH, W), 0.02),
        "w_pool": nrm(ks[11], (DEPTH, POOL_GROUPS, POOL_GROUP, POOL_GROUP), POOL_GROUP ** -0.5),
        "pool_scale": 1.0 + nrm(ks[12], (DEPTH, W), 0.1),
        "mem_norm_g": 1.0 + nrm(ks[13], (DEPTH, D_MODEL), 0.02),
        "w_mem_kv": nrm(ks[14], (DEPTH, D_MODEL, 2 * W), D_MODEL ** -0.5),
        "w_branch": nrm(ks[15], (DEPTH, N_BRANCH, W, D_MODEL), W ** -0.5),
        "w_out": nrm(ks[16], (DEPTH, D_MODEL, D_MODEL), D_MODEL ** -0.5),
        "final_g": 1.0 + nrm(ks[17], (D_MODEL,), 0.02),
    }


def reference(x, mem, rel_bias, norm_g, w_in, gate_b, lam_q1, lam_k1, lam_q2, lam_k2,
              da_norm_g, w_pool, pool_scale, mem_norm_g, w_mem_kv, w_branch, w_out, final_g):
    for l in range(DEPTH):
        x = hybrid_layer(x, mem, l, rel_bias, norm_g[l], w_in[l], gate_b[l], lam_q1[l], lam_k1[l],
                         lam_q2[l], lam_k2[l], da_norm_g[l], w_pool[l], pool_scale[l],
                         mem_norm_g[l], w_mem_kv[l], w_branch[l], w_out[l])
    return rmsnorm(x, final_g)
```

```python
import math
from contextlib import ExitStack

import numpy as np
import ml_dtypes

import concourse.bass as bass
import concourse.mybir as mybir
from concourse.bass_utils import run_bass_kernel_spmd

F32 = mybir.dt.float32
BF16 = mybir.dt.bfloat16
U8 = mybir.dt.uint8
AF = mybir.ActivationFunctionType
ALU = mybir.AluOpType
AX = mybir.AxisListType

S = 4096
D = 2048
W = 1024
DEPTH = 2
NCB = 160
EPS = 1e-6
NEG = -30000.0
GL = 1152
ENG = ("pe", "act", "dve", "pool", "sp")
N_CORES = 8
PAIRS = [[0, 1], [2, 3], [4, 5], [6, 7]]
NPB = 84


class Buf:
    __slots__ = ("name", "w", "r")

    def __init__(self, name=""):
        self.name = name
        self.w = {}
        self.r = {}


class DmaSem:
    __slots__ = ("key", "count")

    def __init__(self, key):
        self.key = key
        self.count = 0


class Prog:
    def __init__(self):
        self.ops = {e: [] for e in ENG}
        self.cnt = dict.fromkeys(ENG, 0)
        self.known = {e: {} for e in ENG}
        self.dsems = []
        self.csems = []
        self.swsems = []
        self.sem_i = 0
        self.swsem_i = 0

    def dma_sem(self, sw=False):
        if sw:
            if self.swsem_i == len(self.swsems):
                self.swsems.append(DmaSem("w_%d" % self.swsem_i))
            s = self.swsems[self.swsem_i]
            self.swsem_i += 1
            return s
        if self.sem_i == len(self.dsems):
            self.dsems.append(DmaSem("d_%d" % self.sem_i))
        s = self.dsems[self.sem_i]
        self.sem_i += 1
        return s

    def _waits(self, eng, reads, writes):
        deps = {}
        for b in reads:
            for k, v in b.w.items():
                if deps.get(k, 0) < v:
                    deps[k] = v
        for b in writes:
            for k, v in b.w.items():
                if deps.get(k, 0) < v:
                    deps[k] = v
            for k, v in b.r.items():
                if deps.get(k, 0) < v:
                    deps[k] = v
        kn = self.known[eng]
        waits = []
        for k, v in deps.items():
            if eng == "pe" and k == "pe":
                continue
            if kn.get(k, 0) < v:
                kn[k] = v
                waits.append((k, v))
        return waits

    def op(self, eng, fns, reads=(), writes=()):
        if not isinstance(fns, (list, tuple)):
            fns = [fns]
        waits = self._waits(eng, reads, writes)
        self.cnt[eng] += 1
        v = self.cnt[eng]
        for b in reads:
            b.r[eng] = v
        for b in writes:
            b.w = {eng: v}
            b.r = {}
        n = len(fns)
        for i, fn in enumerate(fns):
            self.ops[eng].append((waits if i == 0 else (), fn, (eng, 1) if i == n - 1 else None))

    def dma(self, q, sem, out, in_, reads=(), writes=()):
        assert (q == "pool") == sem.key.startswith("w_"), (q, sem.key)
        waits = self._waits(q, reads, writes)
        sem.count += 16
        v = sem.count
        for b in reads:
            b.r[sem.key] = v
        for b in writes:
            b.w = {sem.key: v}
            b.r = {}
        self.ops[q].append((waits, lambda e, o=out, i=in_: e.dma_start(out=o, in_=i), (sem.key, 16)))

    def coll_sem(self):
        s = DmaSem("c_%d" % len(self.csems))
        self.csems.append(s)
        return s

    def coll(self, sem, kind, groups, out, in_, reads=(), writes=()):
        waits = self._waits("pool", reads, writes)
        sem.count += 1
        v = sem.count
        for b in reads:
            b.r[sem.key] = v
        for b in writes:
            b.w = {sem.key: v}
            b.r = {}
        self.ops["pool"].append((waits, lambda e, o=out, i=in_: e.collective_compute(
            kind, ALU.bypass, replica_groups=groups, ins=[i], outs=[o]), (sem.key, 1)))

    def barrier(self, final=False):
        snap = {e: self.cnt[e] for e in ENG if self.cnt[e] > 0}
        for s in self.dsems + self.swsems + (self.csems if final else []):
            if s.count > 0:
                snap[s.key] = s.count
        for e in ENG:
            kn = self.known[e]
            waits = []
            for k, v in snap.items():
                if kn.get(k, 0) < v:
                    kn[k] = v
                    waits.append((k, v))
            if waits:
                self.ops[e].append((waits, None, None))
        self.sem_i = 0
        self.swsem_i = 0

    def emit(self, nc, stack):
        handles = {}
        for e in ENG:
            handles[e] = stack.enter_context(nc.semaphore("s_" + e))
        for s in self.dsems + self.swsems + self.csems:
            handles[s.key] = stack.enter_context(nc.semaphore(s.key))
        block = stack.enter_context(nc.Block())
        ops = self.ops

        def run(engine, name):
            for waits, fn, inc in ops[name]:
                for k, v in waits:
                    engine.wait_ge(handles[k], v)
                if fn is not None:
                    ins = fn(engine)
                    if inc is not None:
                        ins.then_inc(handles[inc[0]], inc[1])

        block.tensor(lambda e: run(e, "pe"))
        block.scalar(lambda e: run(e, "act"))
        block.vector(lambda e: run(e, "dve"))
        block.gpsimd(lambda e: run(e, "pool"))
        block.sync(lambda e: run(e, "sp"))
        return {k: len(v) for k, v in ops.items()}


class Ring:
    def __init__(self, items):
        self.items = items
        self.i = 0

    def next(self):
        it = self.items[self.i % len(self.items)]
        self.i += 1
        return it


class Arena:
    def __init__(self, ap, size):
        self.ap = ap
        self.size = size
        self.off = 0

    def alloc(self, free_shape, dt):
        esz = 2 if dt == BF16 else 4
        n = int(np.prod(free_shape)) * esz
        n_al = (n + 63) // 64 * 64
        assert self.off + n_al <= self.size, f"arena overflow {self.off}+{n_al}>{self.size}"
        v = self.ap[:, self.off:self.off + n].bitcast(dt)
        self.off += n_al
        if len(free_shape) == 2:
            v = v.rearrange("p (a b) -> p a b", b=free_shape[1])
        elif len(free_shape) == 3:
            v = v.rearrange("p (a b c) -> p a b c", b=free_shape[1], c=free_shape[2])
        return v


def MM(out, lhsT, rhs, start, stop):
    return lambda e: e.matmul(out, lhsT, rhs, start=start, stop=stop)


def TR(out, in_, ident):
    return lambda e: e.transpose(out, in_, ident)


def ACT(out, in_, func, **kw):
    return lambda e: e.activation(out=out, in_=in_, func=func, **kw)


def TT(out, in0, in1, op):
    return lambda e: e.tensor_tensor(out=out, in0=in0, in1=in1, op=op)


def STT(out, in0, scalar, in1, op0, op1):
    return lambda e: e.scalar_tensor_tensor(out=out, in0=in0, scalar=scalar, in1=in1, op0=op0, op1=op1)


def TS(out, in0, s1, s2, op0, op1):
    return lambda e: e.tensor_scalar(out=out, in0=in0, scalar1=s1, scalar2=s2, op0=op0, op1=op1)


def TSM(out, in0, s1):
    return lambda e: e.tensor_scalar_mul(out=out, in0=in0, scalar1=s1)


def CP(out, in_):
    return lambda e: e.tensor_copy(out=out, in_=in_)


def RCP(out, in_):
    return lambda e: e.reciprocal(out=out, in_=in_)


def MSET(out, val):
    return lambda e: e.memset(out, val)


def t5_bucket_np(n):
    n = np.maximum(n, 0)
    nf = np.maximum(n, 1).astype(np.float32)
    large = 16 + (np.log(nf / np.float32(16)) / np.float32(math.log(128 / 16)) * np.float32(16)).astype(np.int32)
    large = np.minimum(large, 31)
    return np.where(n < 16, n, large)


DAQ, DAK, DAV, DAZ, SBQ, SBK, SBV, SBZ, PU, PZ, MQ, MZ, GT = 0, 4, 8, 12, 16, 20, 24, 28, 32, 40, 44, 48, 52
SILU_BLOCKS = set(range(12, 16)) | set(range(28, 32)) | set(range(40, 44)) | set(range(48, 52))


def build_program(n_layers=DEPTH, phases="A1234C", debug=False):
    nc = bass.Bass("TRN2", target_bir_lowering=False)
    P = Prog()

    def din(name, shape, dt=F32):
        return nc.dram_tensor(name, shape, dt, kind="ExternalInput").ap()

    x_full = din("x_full", [8, 1024, 1024])
    x_own = din("x_own", [8, 512, 1024])
    mem_in = din("mem", [256, D])
    w_in = din("w_in", [DEPTH, D, NPB * 128])
    w_mkv = din("w_mem_kv", [DEPTH, D, 1024])
    w_br = din("w_branch", [DEPTH, 4, W, 1024])
    w_out = din("w_out", [DEPTH, D, 1024])
    w_pool = din("w_pool", [DEPTH, 4, 256, 128])
    norm_g_bc = din("norm_g_bc", [DEPTH, 128, D])
    memg_bc = din("mem_norm_g_bc", [DEPTH, 128, D])
    fing_bc = din("final_g_bc", [128, 1024])
    gate_b_t = din("gate_b_t", [DEPTH, 128, 32])
    da_g_t = din("da_g_t", [DEPTH, 128, 4])
    psc_t = din("pool_scale_t", [DEPTH, 128, 4])
    lam_bc = din("lam_bc", [DEPTH, 128, 4, 64])
    rel_bias = din("rel_bias", [32, 4])
    rb31_bc = din("rb31_bc", [128, 4])
    c_ident = din("c_ident", [128, 128], BF16)
    c_J = din("c_J", [128, 128], BF16)
    c_trineg = din("c_trineg", [128, 128], BF16)
    c_ones = din("c_ones", [128, 128], BF16)
    c_zeros = din("c_zeros", [128, 128], BF16)
    c_mtri = din("c_mtri", [128, 128], BF16)
    c_negm = din("c_negm", [128, 128], F32)
    c_onesf = din("c_onesf", [128, 128], F32)
    c_ones1f = din("c_ones1f", [128, 128], F32)
    c_oh = din("c_oh", [33, GL], F32)
    c_invc = din("c_invc", [128, 4, 16], F32)

    y_out = nc.dram_tensor("y", [8, 512, 1024], F32, kind="ExternalOutput").ap()
    skind = dict(kind="ExternalOutput") if debug else {}
    proj = nc.dram_tensor("proj", [NPB, 128, S], BF16, **skind).ap()
    br_loc = nc.dram_tensor("br_loc", [16 * 128, S], BF16, **skind).ap()
    br_all = nc.dram_tensor("br_all", [8, 512, S], BF16).ap()
    mg_loc = nc.dram_tensor("mg_loc", [4, 1024, 1024], BF16).ap()
    mg_all = nc.dram_tensor("mg_all", [4, 2048, 1024], BF16).ap()
    xo = nc.dram_tensor("xo", [8, 512, 1024], F32, **skind).ap()
    xf = nc.dram_tensor("xf", [8, 1024, 1024], F32).ap()
    gs_t = nc.dram_tensor("gs", [4, GL], BF16)

    with ExitStack() as st:
        ARENA_BYTES = 200 * 1024
        arena_t = st.enter_context(nc.sbuf_tensor("arena", [128, ARENA_BYTES], U8))
        ps = st.enter_context(nc.psum_tensor("ps", [128, 8, 512], F32))
        A = Arena(arena_t, ARENA_BYTES)
        bank = [ps[:, i, :] for i in range(8)]
        bankB = [Buf(f"bank{i}") for i in range(8)]
        psb = ps[:, 7, :].bitcast(BF16)
        psb6 = ps[:, 6, :].bitcast(BF16)

        brallB = [Buf(f"brall{i}") for i in range(8)]
        mgallB = [Buf(f"mgall{i}") for i in range(4)]
        xfB = [Buf(f"xf{i}") for i in range(8)]
        brS = [P.coll_sem() for _ in range(8)]
        mgS = [P.coll_sem() for _ in range(4)]
        xfS = [P.coll_sem() for _ in range(8)]

        ident = A.alloc([128], BF16)
        Jm = A.alloc([128], BF16)
        trineg = A.alloc([128], BF16)
        ones_bf = A.alloc([128], BF16)
        zeros_bf = A.alloc([128], BF16)
        mtri = A.alloc([128], BF16)
        negm = A.alloc([128], F32)
        ones_f = A.alloc([128], F32)
        ones1_f = A.alloc([128], F32)
        rb31 = A.alloc([4], F32)
        invc = A.alloc([4, 16], F32)
        cs = P.dma_sem()
        for dst, src in ((ident, c_ident), (Jm, c_J), (trineg, c_trineg), (ones_bf, c_ones), (zeros_bf, c_zeros),
                         (mtri, c_mtri), (negm, c_negm), (ones_f, c_onesf), (ones1_f, c_ones1f), (rb31, rb31_bc), (invc, c_invc)):
            P.dma("sp", cs, dst, src)
        P.barrier()
        base_mark = A.off

        if "1" in phases:
            lhs = A.alloc([4], F32)
            oh = A.alloc([GL], F32)
            gsb = A.alloc([GL], BF16)
            tB = Buf("t5")
            ts_ = P.dma_sem()
            P.dma("sp", ts_, lhs[0:32, :], rel_bias, writes=[tB])
            P.dma("sp", ts_, oh[0:33, :], c_oh, writes=[tB])
            P.op("dve", TSM(lhs[0:32, :], lhs[0:32, :], 8.0), reads=[tB], writes=[tB])
            P.op("dve", MSET(lhs[32:33, :], NEG), writes=[tB])
            for i, (c0, c1) in enumerate(((0, 512), (512, 1024), (1024, GL))):
                P.op("pe", MM(bank[i][0:4, 0:c1 - c0], lhs[0:33, :], oh[0:33, c0:c1], True, True), reads=[tB], writes=[bankB[i]])
                P.op("dve", CP(gsb[0:4, c0:c1], bank[i][0:4, 0:c1 - c0]), reads=[bankB[i]], writes=[tB])
            P.dma("sp", ts_, gs_t.ap(), gsb[0:4, :], reads=[tB])
            P.barrier()
            A.off = base_mark

        def phase_A(l, xsrc):
            A.off = base_mark
            hT = A.alloc([16, 2048], BF16)
            wsl = [A.alloc([16, 512], BF16) for _ in range(3)]
            stg = [A.alloc([2048], F32) for _ in range(3)]
            hbf = [A.alloc([2048], BF16) for _ in range(2)]
            gbc = A.alloc([2048], F32)
            gb = A.alloc([32], F32)
            stat = A.alloc([64], F32)
            hTB = [Buf(f"hT{i}") for i in range(4)]
            wslB = [Buf() for _ in range(3)]
            wslS = [P.dma_sem(sw=True) for i in range(3)]
            stgB = [[Buf() for _ in range(4)] for _ in range(3)]
            stgS = [P.dma_sem() for i in range(3)]
            hbfB = [Buf() for _ in range(2)]
            statB = [Buf() for _ in range(16)]
            gB = Buf()
            ms = P.dma_sem()
            P.dma("sp", ms, gbc, norm_g_bc[l], writes=[gB])
            P.dma("sp", ms, gb, gate_b_t[l], writes=[gB])
            w_l = w_in[l].rearrange("(c p) n -> p c n", p=128)
            pring = Ring([0, 1, 2, 3, 4, 5])
            stg_i = 0
            for hf in range(2):
                for tb in range(16):
                    gtb = hf * 16 + tb
                    tc, tq = gtb // 4, gtb % 4
                    s = stg_i % 3
                    stg_i += 1
                    hs = tb % 2
                    sB = stgB[s]
                    for r in range(2):
                        P.dma("sp", stgS[s], stg[s][:, r * 1024:(r + 1) * 1024],
                              xsrc[tc][r * 512 + tq * 128:r * 512 + (tq + 1) * 128, :],
                              reads=[xfB[tc]] if l > 0 else [], writes=sB)
                    c = tb * 3
                    P.op("act", ACT(hbf[hs], stg[s], AF.Square, accum_out=stat[:, c:c + 1]),
                         reads=sB, writes=[hbfB[hs], statB[tb]])
                    P.op("act", ACT(stat[:, c + 1:c + 2], stat[:, c:c + 1], AF.Sqrt, scale=1.0 / D, bias=EPS),
                         reads=[statB[tb]], writes=[statB[tb]])
                    P.op("dve", RCP(stat[:, c + 2:c + 3], stat[:, c + 1:c + 2]), reads=[statB[tb]], writes=[statB[tb]])
                    P.op("dve", STT(hbf[hs], stg[s], stat[:, c + 2:c + 3], gbc, ALU.mult, ALU.mult),
                         reads=sB + [statB[tb], gB], writes=[hbfB[hs]])
                    for g in range(2):
                        pb, pB = (psb6, bankB[6]) if g == 0 else (psb, bankB[7])
                        P.op("pe", [TR(pb[:, i * 128:(i + 1) * 128], hbf[hs][:, (g * 8 + i) * 128:(g * 8 + i + 1) * 128], ident)
                                    for i in range(8)], reads=[hbfB[hs]], writes=[pB])
                        src = pb.rearrange("p (a b) -> p a b", b=128)
                        dst = hT[:, g * 8:(g + 1) * 8, tb * 128:(tb + 1) * 128]
                        if g == 0:
                            P.op("act", ACT(dst, src, AF.Copy), reads=[pB], writes=[hTB[tb // 4]])
                        else:
                            P.op("dve", CP(dst, src), reads=[pB], writes=[hTB[tb // 4]])
                for ws in range(NPB // 4):
                    sl = ws % 3
                    P.dma("pool", wslS[sl], wsl[sl], w_l[:, :, ws * 512:(ws + 1) * 512], writes=[wslB[sl]])
                    for j in range(4):
                        cb = ws * 4 + j
                        s = stg_i % 3
                        stg_i += 1
                        ostg = stg[s].bitcast(BF16)
                        for tcl in range(4):
                            bk = pring.next()
                            P.op("pe", [MM(bank[bk], wsl[sl][:, c, j * 128:(j + 1) * 128], hT[:, c, tcl * 512:(tcl + 1) * 512],
                                           c == 0, c == 15) for c in range(16)],
                                 reads=[wslB[sl], hTB[tcl]], writes=[bankB[bk]])
                            dst = ostg[:, tcl * 512:(tcl + 1) * 512]
                            oB = [stgB[s][tcl]]
                            if cb >= GT:
                                P.op("act", ACT(dst, bank[bk], AF.Sigmoid, bias=gb[:, cb - GT:cb - GT + 1]),
                                     reads=[bankB[bk], gB], writes=oB)
                            elif cb in SILU_BLOCKS:
                                P.op("act", ACT(dst, bank[bk], AF.Silu), reads=[bankB[bk]], writes=oB)
                            elif SBQ <= cb < SBQ + 4:
                                P.op("dve", TSM(dst, bank[bk], 128.0 ** -0.5), reads=[bankB[bk]], writes=oB)
                            elif MQ <= cb < MQ + 4:
                                P.op("dve", TSM(dst, bank[bk], 1.0 / 16.0), reads=[bankB[bk]], writes=oB)
                            else:
                                P.op("dve", CP(dst, bank[bk]), reads=[bankB[bk]], writes=oB)
                        P.dma("sp", stgS[s], proj[cb][:, hf * 2048:(hf + 1) * 2048], ostg[:, 0:2048], reads=stgB[s])
            P.barrier()

        def gather_branch(n):
            for k in range(2):
                i = n * 2 + k
                r0 = (n * 4 + 2 * k) * 128
                P.coll(brS[i], "AllGather", PAIRS, br_all[i], br_loc[r0:r0 + 256, :], writes=[brallB[i]])

        def attn_common_alloc():
            A.off = base_mark
            d = {}
            for nme in ("qT", "kT", "vT", "zT", "ostg"):
                d[nme] = [A.alloc([S], BF16) for _ in range(2)]
            d["vtok"] = [A.alloc([32, 128], BF16) for _ in range(2)]
            return d

        def load_head(d, hs, cbs, hB, hS, transposes=True, ring=None):
            for name, cb in zip(("qT", "kT", "vT", "zT"), cbs):
                P.dma("sp", hS[hs][name], d[name][hs], proj[cb], writes=[hB[hs][name]])
            if transposes:
                head_transposes(d, hs, hB, ring)

        def head_transposes(d, hs, hB, ring=None):
            for g in range(4):
                bk = 7 if ring is None else ring.next()
                pb = ps[:, bk, :].bitcast(BF16)
                P.op("pe", [TR(pb[:, i * 128:(i + 1) * 128], d["vT"][hs][:, (g * 8 + i) * 128:(g * 8 + i + 1) * 128], ident)
                            for i in range(8)], reads=[hB[hs]["vT"]], writes=[bankB[bk]])
                P.op("dve", CP(d["vtok"][hs][:, g * 8:(g + 1) * 8, :], pb.rearrange("p (a b) -> p a b", b=128)),
                     reads=[bankB[bk]], writes=[hB[hs]["vtok"]])

        NH = 4

        def phase_DA(l):
            d = attn_common_alloc()
            lam_init = 0.8 - 0.6 * math.exp(-0.3 * l)
            Tp = A.alloc([NH, 1024], BF16)
            Er = [A.alloc([512], BF16) for _ in range(12)]
            tmp = [[A.alloc([512], F32) for _ in range(3)] for _ in range(2)]
            lamt = A.alloc([4, 64], F32)
            lsm = A.alloc([8], F32)
            gsc = A.alloc([NH], F32)
            names = ("qT", "kT", "vT", "zT", "vtok", "ostg")
            hB = [{n: Buf(n) for n in names} for _ in range(2)]
            hS = [{n: P.dma_sem() for n in names} for i in range(2)]
            ErB = [Buf() for _ in range(12)]
            tmpB = [[Buf() for _ in range(3)] for _ in range(2)]
            sB = Buf()
            ss = P.dma_sem()
            P.dma("sp", ss, lamt, lam_bc[l], writes=[sB])
            P.dma("sp", ss, gsc, da_g_t[l], writes=[sB])
            for h in range(NH):
                P.dma("sp", ss, Tp[:, h, :], bass.AP(gs_t, h * GL, [[1, 128], [1, 1024]]), writes=[sB])
            P.op("dve", TT(lamt[:, 0, :], lamt[:, 0, :], lamt[:, 1, :], ALU.mult), reads=[sB], writes=[sB])
            P.op("dve", TT(lamt[:, 2, :], lamt[:, 2, :], lamt[:, 3, :], ALU.mult), reads=[sB], writes=[sB])
            P.op("dve", lambda e: e.reduce_sum(out=lsm[:, 0:1], in_=lamt[:, 0, :], axis=AX.X), reads=[sB], writes=[sB])
            P.op("dve", lambda e: e.reduce_sum(out=lsm[:, 1:2], in_=lamt[:, 2, :], axis=AX.X), reads=[sB], writes=[sB])
            P.op("act", ACT(lsm[:, 2:4], lsm[:, 0:2], AF.Exp), reads=[sB], writes=[sB])
            P.op("dve", TT(lsm[:, 4:5], lsm[:, 3:4], lsm[:, 2:3], ALU.subtract), reads=[sB], writes=[sB])
            P.op("dve", lambda e: e.tensor_scalar_add(out=lsm[:, 5:6], in0=lsm[:, 4:5], scalar1=-lam_init), reads=[sB], writes=[sB])
            P.op("dve", TSM(gsc, gsc, 1.0 - lam_init), reads=[sB], writes=[sB])
            neglam = lsm[:, 5:6]

            Zc = [[A.alloc([512], F32) for _ in range(3)] for _ in range(2)]
            ZcB = [[Buf() for _ in range(3)] for _ in range(2)]
            acc_sets = ((0, 1), (2, 3))
            sring = Ring([4, 5, 6])
            ering = Ring(list(range(12)))
            load_head(d, 0, (DAQ, DAK, DAV, DAZ), hB, hS, ring=sring)
            for h in range(NH):
                hs = h % 2
                if h + 1 < NH:
                    load_head(d, (h + 1) % 2, (DAQ + h + 1, DAK + h + 1, DAV + h + 1, DAZ + h + 1), hB, hS, transposes=False)
                qT, kT, vtok, zT, ostg = d["qT"][hs], d["kT"][hs], d["vtok"][hs], d["zT"][hs], d["ostg"][hs]
                B = hB[hs]
                tiles = [(qc, kb) for qc in range(8) for kb in range(4 * qc + 4)]
                state = {}
                pending = []

                def stage0(qc, kb):
                    j = kb - 4 * qc
                    qs = max(j, 0) * 128
                    near = j >= -1
                    es = []
                    for c in range(2):
                        bk = sring.next()
                        fns = [MM(bank[bk][:, qs:512], kT[c * 64:(c + 1) * 64, kb * 128:(kb + 1) * 128],
                                  qT[c * 64:(c + 1) * 64, qc * 512 + qs:(qc + 1) * 512], True, not near)]
                        if near:
                            off = 128 * (3 - j)
                            fns.append(MM(bank[bk][:, qs:512], Jm, Tp[:, h, off + qs:off + 512], False, True))
                        P.op("pe", fns, reads=[B["kT"], B["qT"], sB], writes=[bankB[bk]])
                        ei = ering.next()
                        bias = 0.0 if near else rb31[:, h:h + 1]
                        P.op("act", ACT(Er[ei][:, qs:512], bank[bk][:, qs:512], AF.Exp, scale=0.125, bias=bias),
                             reads=[bankB[bk]], writes=[ErB[ei]])
                        es.append(ei)
                    state[(qc, kb)] = (qs, es)

                def stage1(qc, kb):
                    qs, es = state.pop((qc, kb))
                    first = kb == 0
                    last = kb == 4 * qc + 3
                    k = qc % 2
                    acc_o = acc_sets[k]
                    if first:
                        P.op("dve", MSET(Zc[k][1], 0.0), writes=[ZcB[k][1]])
                    for c in range(2):
                        E = Er[es[c]]
                        if c == 0:
                            P.op("pe", [MM(bank[acc_o[0]][:, qs:512], vtok[:, kb, :], E[:, qs:512], first, last),
                                        MM(bank[7][:, qs:512], ones_bf, E[:, qs:512], first, last)],
                                 reads=[B["vtok"], ErB[es[0]]], writes=[bankB[acc_o[0]], bankB[7]])
                            continue
                        P.op("pe", MM(bank[acc_o[c]][:, qs:512], vtok[:, kb, :], E[:, qs:512], first, last),
                             reads=[B["vtok"], ErB[es[c]]], writes=[bankB[acc_o[c]]])
                        if kb % 2 == 0:
                            eng, zi = "pool", 2
                        else:
                            eng, zi = "dve", 1
                        if first:
                            P.op(eng, CP(Zc[k][zi], E), reads=[ErB[es[c]]], writes=[ZcB[k][zi]])
                        else:
                            P.op(eng, TT(Zc[k][zi][:, qs:512], Zc[k][zi][:, qs:512], E[:, qs:512], ALU.add),
                                 reads=[ErB[es[c]], ZcB[k][zi]], writes=[ZcB[k][zi]])
                    if last:
                        t0_, b0_ = tmp[k][0], tmpB[k][0]
                        P.op("dve", RCP(t0_, bank[7]), reads=[bankB[7]], writes=[b0_])
                        P.op("dve", TT(t0_, bank[acc_o[0]], t0_, ALU.mult), reads=[bankB[acc_o[0]], b0_], writes=[b0_])
                        pending.append([2, lambda qc=qc: combine(qc)])

                def combine(qc):
                    k = qc % 2
                    acc_o = acc_sets[k]
                    t0, t1, t2 = tmp[k]
                    b0, b1, b2 = tmpB[k]
                    z1 = sring.next()
                    P.op("pe", [MM(bank[z1], ones1_f, Zc[k][1], True, False), MM(bank[z1], ones1_f, Zc[k][2], False, True)],
                         reads=[ZcB[k][1], ZcB[k][2]], writes=[bankB[z1]])
                    P.op("dve", RCP(t1, bank[z1]), reads=[bankB[z1]], writes=[b1])
                    P.op("dve", TT(t1, bank[acc_o[1]], t1, ALU.mult), reads=[bankB[acc_o[1]], b1], writes=[b1])
                    P.op("dve", STT(t0, t1, neglam, t0, ALU.mult, ALU.add), reads=[b0, b1, sB], writes=[b0])
                    P.op("dve", TT(t2, t0, t0, ALU.mult), reads=[b0], writes=[b2])

                    def part2():
                        bk = sring.next()
                        P.op("pe", MM(bank[bk], ones_f, t2, True, True), reads=[b2], writes=[bankB[bk]])
                        P.op("act", ACT(t2, bank[bk], AF.Ln, bias=EPS), reads=[bankB[bk]], writes=[b2])
                        P.op("act", ACT(t2, t2, AF.Exp, scale=-0.5), reads=[b2], writes=[b2])
                        P.op("dve", TT(t0, t0, t2, ALU.mult), reads=[b0, b2], writes=[b0])
                        P.op("dve", STT(ostg[:, qc * 512:(qc + 1) * 512], t0, gsc[:, h:h + 1], zT[:, qc * 512:(qc + 1) * 512],
                                        ALU.mult, ALU.mult), reads=[b0, sB, B["zT"]], writes=[B["ostg"]])
                    pending.append([3, part2])

                n = len(tiles)
                for s_ in range(n + 1):
                    if s_ < n:
                        stage0(*tiles[s_])
                    if s_ >= 1:
                        stage1(*tiles[s_ - 1])
                    if s_ == n // 2 and h + 1 < NH:
                        head_transposes(d, (h + 1) % 2, hB, sring)
                    for it in pending:
                        it[0] -= 1
                    for it in [it for it in pending if it[0] <= 0]:
                        pending.remove(it)
                        it[1]()
                while pending:
                    it = pending.pop(0)
                    it[1]()
                P.dma("sp", hS[hs]["ostg"], br_loc[(0 + h) * 128:(1 + h) * 128, :], ostg, reads=[B["ostg"]])
            P.barrier()
            gather_branch(0)

        def phase_SB(l):
            d = attn_common_alloc()
            ebuf = [A.alloc([512], F32) for _ in range(2)]
            spb = [A.alloc([512], BF16) for _ in range(3)]
            argb = [A.alloc([512], F32) for _ in range(2)]
            Ab = [A.alloc([512], BF16) for _ in range(3)]
            Rsb = [A.alloc([512], F32) for _ in range(2)]
            names = ("qT", "kT", "vT", "zT", "vtok", "ostg")
            hB = [{n: Buf(n) for n in names} for _ in range(2)]
            hS = [{n: P.dma_sem() for n in names} for i in range(2)]
            ebufB = [Buf() for _ in range(2)]
            spB = [Buf() for _ in range(3)]
            argB = [Buf() for _ in range(2)]
            AbB = [Buf() for _ in range(3)]
            RsB = [Buf() for _ in range(2)]
            zring = Ring([0, 1, 2])
            cring = Ring([3, 4])
            acc = (5, 6)
            e_r, sp_r, arg_r, A_r = Ring([0, 1]), Ring([0, 1, 2]), Ring([0, 1]), Ring([0, 1, 2])
            load_head(d, 0, (SBQ, SBK, SBV, SBZ), hB, hS)
            chunk_ctr = [0]
            for h in range(NH):
                hs = h % 2
                if h + 1 < NH:
                    load_head(d, (h + 1) % 2, (SBQ + h + 1, SBK + h + 1, SBV + h + 1, SBZ + h + 1), hB, hS, transposes=False)
                qT, kT, vtok, zT, ostg = d["qT"][hs], d["kT"][hs], d["vtok"][hs], d["zT"][hs], d["ostg"][hs]
                B = hB[hs]
                tiles = [(qc, kb) for qc in range(8) for kb in reversed(range(4 * qc + 4))]
                state = {}

                def stage0(qc, kb):
                    j = kb - 4 * qc
                    qs = max(j, 0) * 128
                    bk = zring.next()
                    P.op("pe", MM(bank[bk][:, qs:512], kT[:, kb * 128:(kb + 1) * 128], qT[:, qc * 512 + qs:(qc + 1) * 512],
                                  True, True), reads=[B["kT"], B["qT"]], writes=[bankB[bk]])
                    ei, si = e_r.next(), sp_r.next()
                    P.op("act", ACT(ebuf[ei][:, qs:512], bank[bk][:, qs:512], AF.Exp), reads=[bankB[bk]], writes=[ebufB[ei]])
                    P.op("act", ACT(spb[si][:, qs:512], ebuf[ei][:, qs:512], AF.Ln, bias=1.0), reads=[ebufB[ei]], writes=[spB[si]])
                    if j >= 0:
                        P.op("dve", TT(spb[si][:, qs:qs + 128], spb[si][:, qs:qs + 128], mtri, ALU.mult),
                             reads=[spB[si]], writes=[spB[si]])
                    state[(qc, kb)] = dict(qs=qs, bk=bk, si=si, j=j)

                def stage1(qc, kb):
                    stt_ = state[(qc, kb)]
                    qs, bk, si, j = stt_["qs"], stt_["bk"], stt_["si"], stt_["j"]
                    first = kb == 4 * qc + 3
                    if first:
                        chunk_ctr[0] += 1
                    R = Rsb[chunk_ctr[0] % 2]
                    RB = RsB[chunk_ctr[0] % 2]
                    if first:
                        P.op("dve", MSET(R, 0.0), writes=[RB])
                    ck = cring.next()
                    P.op("pe", [MM(bank[bk][:, qs:512], trineg, spb[si][:, qs:512], False, True),
                                MM(bank[ck][:, qs:512], ones_bf, spb[si][:, qs:512], True, True)],
                         reads=[spB[si]], writes=[bankB[bk], bankB[ck]])
                    ai = arg_r.next()
                    P.op("dve", TT(argb[ai][:, qs:512], bank[bk][:, qs:512], R[:, qs:512], ALU.subtract),
                         reads=[bankB[bk], RB], writes=[argB[ai]])
                    if j >= 0:
                        P.op("dve", TT(argb[ai][:, qs:qs + 128], argb[ai][:, qs:qs + 128], negm, ALU.add),
                             reads=[argB[ai]], writes=[argB[ai]])
                    if kb > 0:
                        P.op("dve", TT(R[:, qs:512], R[:, qs:512], bank[ck][:, qs:512], ALU.add), reads=[bankB[ck], RB], writes=[RB])
                    Ai = A_r.next()
                    P.op("act", ACT(Ab[Ai][:, qs:512], argb[ai][:, qs:512], AF.Exp), reads=[argB[ai]], writes=[AbB[Ai]])
                    stt_["Ai"] = Ai

                def stage2(qc, kb):
                    stt_ = state.pop((qc, kb))
                    qs, Ai = stt_["qs"], stt_["Ai"]
                    a = acc[qc % 2]
                    first = kb == 4 * qc + 3
                    last = kb == 0
                    fns = []
                    if first:
                        fns.append(MM(bank[a], zeros_bf, qT[:, 0:512], True, False))
                    fns.append(MM(bank[a][:, qs:512], vtok[:, kb, :], Ab[Ai][:, qs:512], False, last))
                    P.op("pe", fns, reads=[B["vtok"], AbB[Ai], B["qT"]], writes=[bankB[a]])
                    if last:
                        P.op("dve", TT(ostg[:, qc * 512:(qc + 1) * 512], bank[a], zT[:, qc * 512:(qc + 1) * 512], ALU.mult),
                             reads=[bankB[a], B["zT"]], writes=[B["ostg"]])

                n = len(tiles)
                for s_ in range(n + 2):
                    if s_ < n:
                        stage0(*tiles[s_])
                    if 1 <= s_ <= n:
                        stage1(*tiles[s_ - 1])
                    if s_ >= 2:
                        stage2(*tiles[s_ - 2])
                    if s_ == n // 2 and h + 1 < NH:
                        head_transposes(d, (h + 1) % 2, hB)
                P.dma("sp", hS[hs]["ostg"], br_loc[(4 + h) * 128:(5 + h) * 128, :], ostg, reads=[B["ostg"]])
            P.barrier()
            gather_branch(1)

        def phase_pool(l):
            A.off = base_mark
            PADL = 16
            ubf = [A.alloc([S], BF16) for _ in range(2)]
            pz = A.alloc([S], BF16)
            pooled = [A.alloc([S], BF16) for _ in range(2)]
            X = [A.alloc([PADL + S], F32) for _ in range(3)]
            ostg = A.alloc([S], BF16)
            wp = A.alloc([2, 128], BF16)
            psc = A.alloc([4], F32)
            t16 = A.alloc([16], F32)
            ubB = [Buf() for _ in range(2)]
            pzB = Buf()
            poB = [Buf() for _ in range(2)]
            XB = [Buf() for _ in range(3)]
            osB = Buf()
            wpB, pscB, t16B = Buf(), Buf(), Buf()
            sems = {k: P.dma_sem(sw=(k == "w")) for k in ("u0", "u1", "z", "o", "w", "m")}
            P.dma("sp", sems["m"], psc, psc_t[l], writes=[pscB])
            for i in range(3):
                P.op("dve", MSET(X[i][:, 0:PADL], 0.0), writes=[XB[i]])
            pr = Ring([0, 1, 2, 3, 4, 5])
            for g in range(4):
                wwin = 2 ** (g + 1)
                P.dma("pool", sems["w"], wp, w_pool[l, g].rearrange("(c p) n -> p c n", p=128), writes=[wpB])
                for cb in range(2):
                    P.dma("sp", sems[f"u{cb}"], ubf[cb], proj[PU + 2 * g + cb], writes=[ubB[cb]])
                P.dma("sp", sems["z"], pz, proj[PZ + g], writes=[pzB])
                for cb in range(2):
                    for hh in range(2):
                        sl_ = slice(PADL + hh * 2048, PADL + (hh + 1) * 2048)
                        P.op("dve", CP(X[0][:, sl_], ubf[cb][:, hh * 2048:(hh + 1) * 2048]), reads=[ubB[cb]], writes=[XB[0]])
                    cur = 0
                    for lev in range(g + 1):
                        sh = 2 ** lev
                        nxt = 1 if cur != 1 else 2
                        for hh in range(2):
                            a0 = PADL + hh * 2048
                            P.op("dve", TT(X[nxt][:, a0:a0 + 2048], X[cur][:, a0:a0 + 2048], X[cur][:, a0 - sh:a0 + 2048 - sh], ALU.add),
                                 reads=[XB[cur]], writes=[XB[nxt]])
                        cur = nxt
                    for hh in range(2):
                        a0 = PADL + hh * 2048
                        P.op("dve", STT(pooled[cb][:, hh * 2048:(hh + 1) * 2048], X[cur][:, a0:a0 + 2048], 1.0 / wwin,
                                        X[0][:, a0:a0 + 2048], ALU.mult, ALU.subtract), reads=[XB[cur], XB[0]], writes=[poB[cb]])
                    P.op("dve", TT(t16, X[cur][:, PADL:PADL + 16], invc[:, g, :], ALU.mult), reads=[XB[cur]], writes=[t16B])
                    P.op("dve", TT(pooled[cb][:, 0:16], t16, X[0][:, PADL:PADL + 16], ALU.subtract), reads=[t16B, XB[0]], writes=[poB[cb]])
                for tc in range(8):
                    bk = pr.next()
                    P.op("pe", [MM(bank[bk], wp[:, c, :], pooled[c][:, tc * 512:(tc + 1) * 512], c == 0, c == 1)
                                for c in range(2)], reads=[wpB, poB[0], poB[1]], writes=[bankB[bk]])
                    P.op("dve", STT(ostg[:, tc * 512:(tc + 1) * 512], bank[bk], psc[:, g:g + 1],
                                    pz[:, tc * 512:(tc + 1) * 512], ALU.mult, ALU.mult),
                         reads=[bankB[bk], pscB, pzB], writes=[osB])
                P.dma("sp", sems["o"], br_loc[(8 + g) * 128:(9 + g) * 128, :], ostg, reads=[osB])
            P.barrier()
            gather_branch(2)

        def phase_mem(l):
            A.off = base_mark
            mstg = [A.alloc([D], F32) for _ in range(2)]
            mh = [A.alloc([D], BF16) for _ in range(2)]
            gbc = A.alloc([D], F32)
            stat = A.alloc([8], F32)
            memT = A.alloc([16, 256], BF16)
            wsl = [A.alloc([16, 512], BF16) for _ in range(2)]
            mkT = A.alloc([4, 256], BF16)
            mv = A.alloc([2, 512], BF16)
            qT = [A.alloc([S], BF16) for _ in range(2)]
            mz = [A.alloc([S], BF16) for _ in range(2)]
            ostg = [A.alloc([S], BF16) for _ in range(2)]
            Eb = [A.alloc([512], BF16) for _ in range(2)]
            tm = [A.alloc([512], F32) for _ in range(2)]
            mB = [Buf() for _ in range(2)]
            mhB = [Buf() for _ in range(2)]
            gB, stB, memTB, mkB, mvB = Buf(), Buf(), Buf(), Buf(), Buf()
            wB = [Buf() for _ in range(2)]
            qB = [Buf() for _ in range(2)]
            zB = [Buf() for _ in range(2)]
            oB = [Buf() for _ in range(2)]
            EB = [Buf() for _ in range(2)]
            tB = [Buf() for _ in range(2)]
            sm = {k: P.dma_sem(sw=k.startswith("w")) for k in ("m0", "m1", "g", "w0", "w1", "q0", "q1", "z0", "z1", "o0", "o1")}
            P.dma("sp", sm["g"], gbc, memg_bc[l], writes=[gB])
            for mb in range(2):
                P.dma("sp", sm[f"m{mb}"], mstg[mb], mem_in[mb * 128:(mb + 1) * 128, :], writes=[mB[mb]])
                c = mb * 3
                P.op("act", ACT(mh[mb], mstg[mb], AF.Square, accum_out=stat[:, c:c + 1]), reads=[mB[mb]], writes=[mhB[mb], stB])
                P.op("act", ACT(stat[:, c + 1:c + 2], stat[:, c:c + 1], AF.Sqrt, scale=1.0 / D, bias=EPS), reads=[stB], writes=[stB])
                P.op("dve", RCP(stat[:, c + 2:c + 3], stat[:, c + 1:c + 2]), reads=[stB], writes=[stB])
                P.op("dve", STT(mh[mb], mstg[mb], stat[:, c + 2:c + 3], gbc, ALU.mult, ALU.mult),
                     reads=[mB[mb], stB, gB], writes=[mhB[mb]])
                for g in range(2):
                    P.op("pe", [TR(psb[:, i * 128:(i + 1) * 128], mh[mb][:, (g * 8 + i) * 128:(g * 8 + i + 1) * 128], ident)
                                for i in range(8)], reads=[mhB[mb]], writes=[bankB[7]])
                    P.op("dve", CP(memT[:, g * 8:(g + 1) * 8, mb * 128:(mb + 1) * 128], psb.rearrange("p (a b) -> p a b", b=128)),
                         reads=[bankB[7]], writes=[memTB])
            wv = w_mkv[l].rearrange("(c p) n -> p c n", p=128)
            pr = Ring([0, 1, 2, 3, 4, 5])
            for ws in range(2):
                sl = ws
                P.dma("pool", sm[f"w{sl}"], wsl[sl], wv[:, :, ws * 512:(ws + 1) * 512], writes=[wB[sl]])
                if ws == 0:
                    for j in range(4):
                        bk = pr.next()
                        P.op("pe", [MM(bank[bk][:, 0:256], wsl[sl][:, c, j * 128:(j + 1) * 128], memT[:, c, :], c == 0, c == 15)
                                    for c in range(16)], reads=[wB[sl], memTB], writes=[bankB[bk]])
                        P.op("dve", CP(mkT[:, j, :], bank[bk][:, 0:256]), reads=[bankB[bk]], writes=[mkB])
                else:
                    for mb in range(2):
                        bk = pr.next()
                        P.op("pe", [MM(bank[bk], memT[:, c, mb * 128:(mb + 1) * 128], wsl[sl][:, c, :], c == 0, c == 15)
                                    for c in range(16)], reads=[wB[sl], memTB], writes=[bankB[bk]])
                        P.op("dve", CP(mv[:, mb, :], bank[bk]), reads=[bankB[bk]], writes=[mvB])
            for hm in range(2):
                for dc in range(2):
                    P.dma("sp", sm[f"q{dc}"], qT[dc], proj[MQ + 2 * hm + dc], writes=[qB[dc]])
                    P.dma("sp", sm[f"z{dc}"], mz[dc], proj[MZ + 2 * hm + dc], writes=[zB[dc]])
                for tc in range(8):
                    csl = slice(tc * 512, (tc + 1) * 512)
                    for mb in range(2):
                        bk = pr.next()
                        P.op("pe", [MM(bank[bk], mkT[:, 2 * hm + dc, mb * 128:(mb + 1) * 128], qT[dc][:, csl], dc == 0, dc == 1)
                                    for dc in range(2)], reads=[mkB, qB[0], qB[1]], writes=[bankB[bk]])
                        P.op("act", ACT(Eb[mb], bank[bk], AF.Exp), reads=[bankB[bk]], writes=[EB[mb]])
                    zk = pr.next()
                    P.op("pe", [MM(bank[zk], ones_bf, Eb[mb], mb == 0, mb == 1) for mb in range(2)],
                         reads=[EB[0], EB[1]], writes=[bankB[zk]])
                    P.op("dve", RCP(tm[0], bank[zk]), reads=[bankB[zk]], writes=[tB[0]])
                    for db in range(2):
                        ok = pr.next()
                        c0 = hm * 256 + db * 128
                        P.op("pe", [MM(bank[ok], mv[:, mb, c0:c0 + 128], Eb[mb], mb == 0, mb == 1) for mb in range(2)],
                             reads=[mvB, EB[0], EB[1]], writes=[bankB[ok]])
                        P.op("dve", TT(tm[1], bank[ok], tm[0], ALU.mult), reads=[bankB[ok], tB[0]], writes=[tB[1]])
                        P.op("dve", TT(ostg[db][:, csl], tm[1], mz[db][:, csl], ALU.mult), reads=[tB[1], zB[db]], writes=[oB[db]])
                for db in range(2):
                    P.dma("sp", sm[f"o{db}"], br_loc[(12 + 2 * hm + db) * 128:(13 + 2 * hm + db) * 128, :], ostg[db], reads=[oB[db]])
            P.barrier()
            gather_branch(3)

        def phase_C(l, xown_src, final):
            A.off = base_mark
            brc2 = [A.alloc([4, 8, 512], BF16) for _ in range(2)]
            sg = [A.alloc([4, 512], BF16) for _ in range(3)]
            mT = [A.alloc([8, 512], BF16) for _ in range(2)]
            wbr = A.alloc([4, 8, 1024], BF16)
            tm = [A.alloc([512], F32) for _ in range(4)]
            brB2 = [[Buf() for _ in range(4)] for _ in range(2)]
            sgB = [Buf() for _ in range(3)]
            mTB = [Buf() for _ in range(2)]
            wbB = [[Buf() for _ in range(2)] for _ in range(4)]
            tB = [Buf() for _ in range(4)]
            mglocB = [Buf() for _ in range(4)]
            sm = {k: P.dma_sem() for k in ["b0", "b1", "b2", "b3", "c0", "c1", "c2", "c3", "s0", "s1", "s2", "m0", "m1"]}
            wsm = [[P.dma_sem(sw=True) for _ in range(2)] for _ in range(4)]
            for dg in range(2):
                for n in range(4):
                    P.dma("pool", wsm[n][dg], wbr[:, n, :, dg * 512:(dg + 1) * 512],
                          w_br[l, n].rearrange("(w p) d -> p w d", p=128)[:, :, dg * 512:(dg + 1) * 512], writes=[wbB[n][dg]])
            sg_i = 0
            for tc in range(8):
                csl = slice(tc * 512, (tc + 1) * 512)
                m = mT[tc % 2]
                mB = mTB[tc % 2]
                brc = brc2[tc % 2]
                brB = brB2[tc % 2]
                for n in range(4):
                    P.dma("sp", sm[("b%d" if tc % 2 == 0 else "c%d") % n], brc[:, n, :, :],
                          br_all[n * 2:(n + 1) * 2].rearrange("k (j p) t -> p (k j) t", p=128)[:, :, csl],
                          reads=[brallB[2 * n], brallB[2 * n + 1]], writes=[brB[n]])
                for dg in range(2):
                    for jj in range(4):
                        dl = dg * 4 + jj
                        si = sg_i % 3
                        sg_i += 1
                        P.dma("sp", sm[f"s{si}"], sg[si], proj[GT + dl:GT + 32:8, :, csl].rearrange("n p t -> p n t"), writes=[sgB[si]])
                        bset = (0, 1, 2, 3) if dl % 2 == 0 else (4, 5, 6, 7)
                        for n in range(4):
                            P.op("pe", [MM(bank[bset[n]], wbr[:, n, wc, dl * 128:(dl + 1) * 128], brc[:, n, wc, :], wc == 0, wc == 7)
                                        for wc in range(8)], reads=[wbB[n][dg], brB[n]], writes=[bankB[bset[n]]])
                        for n in range(4):
                            P.op("dve", TT(tm[n], bank[bset[n]], sg[si][:, n, :], ALU.mult), reads=[bankB[bset[n]], sgB[si]], writes=[tB[n]])
                        P.op("pool", TT(tm[0], tm[0], tm[1], ALU.add), reads=[tB[0], tB[1]], writes=[tB[0]])
                        P.op("pool", TT(tm[2], tm[2], tm[3], ALU.add), reads=[tB[2], tB[3]], writes=[tB[2]])
                        P.op("pool", TT(m[:, dl, :], tm[0], tm[2], ALU.add), reads=[tB[0], tB[2]], writes=[mB])
                q = tc // 2
                P.dma("sp", sm[f"m{tc % 2}"], mg_loc[q].rearrange("(d p) t -> p d t", p=128)[:, :, (tc % 2) * 512:(tc % 2 + 1) * 512],
                      m, reads=[mB], writes=[mglocB[q]])
                if tc % 2 == 1:
                    P.coll(mgS[q], "AllGather", PAIRS, mg_all[q], mg_loc[q], reads=[mglocB[q]], writes=[mgallB[q]])
            P.barrier()
            A.off = base_mark
            wo_sb = [A.alloc([16, 512], BF16) for _ in range(2)]
            mall = [A.alloc([16, 512], BF16) for _ in range(2)]
            xt = [A.alloc([1024], F32) for _ in range(4)]
            woB = Buf()
            maB = [Buf() for _ in range(2)]
            xB = [Buf() for _ in range(4)]
            xoB = [[Buf() for _ in range(4)] for _ in range(8)]
            sm = {k: P.dma_sem(sw=(k == "w")) for k in ["w", "a0", "a1", "x0", "x1", "x2", "x3"]}
            wo = w_out[l].rearrange("(c p) n -> p c n", p=128)
            for cg in range(2):
                P.dma("pool", sm["w"], wo_sb[cg], wo[:, :, cg * 512:(cg + 1) * 512], writes=[woB])
            yr = Ring([0, 1, 2, 3, 4, 5, 6, 7])
            for tc in range(8):
                q = tc // 2
                ma = mall[tc % 2]
                P.dma("sp", sm[f"a{tc % 2}"], ma, mg_all[q].rearrange("(c p) t -> p c t", p=128)[:, :, (tc % 2) * 512:(tc % 2 + 1) * 512],
                      reads=[mgallB[q]], writes=[maB[tc % 2]])
                for tb in range(4):
                    P.dma("sp", sm[f"x{tb}"], xt[tb], xown_src[tc][tb * 128:(tb + 1) * 128, :], writes=[xB[tb]])
                    for cg in range(2):
                        bk = yr.next()
                        P.op("pe", [MM(bank[bk], ma[:, c, tb * 128:(tb + 1) * 128], wo_sb[cg][:, c, :], c == 0, c == 15) for c in range(16)],
                             reads=[maB[tc % 2], woB], writes=[bankB[bk]])
                        P.op("dve", TT(xt[tb][:, cg * 512:(cg + 1) * 512], xt[tb][:, cg * 512:(cg + 1) * 512], bank[bk], ALU.add),
                             reads=[bankB[bk], xB[tb]], writes=[xB[tb]])
                    P.dma("sp", sm[f"x{tb}"], xo[tc][tb * 128:(tb + 1) * 128, :], xt[tb], reads=[xB[tb]], writes=[xoB[tc][tb]])
                P.coll(xfS[tc], "AllGather", PAIRS, xf[tc], xo[tc], reads=xoB[tc], writes=[xfB[tc]])
            P.barrier()
            if final:
                A.off = base_mark
                fg = A.alloc([1024], F32)
                fl = [A.alloc([D], F32) for _ in range(2)]
                ow = [A.alloc([1024], F32) for _ in range(2)]
                junk = A.alloc([D], BF16)
                stat = A.alloc([8], F32)
                fgB, jB = Buf(), Buf()
                flB = [Buf() for _ in range(2)]
                owB = [Buf() for _ in range(2)]
                stB = [Buf() for _ in range(2)]
                sm = {k: P.dma_sem() for k in ["g", "f0", "f1", "o0", "o1"]}
                P.dma("sp", sm["g"], fg, fing_bc, writes=[fgB])
                for gtb in range(32):
                    tc, tq = gtb // 4, gtb % 4
                    k = gtb % 2
                    for r in range(2):
                        P.dma("sp", sm[f"f{k}"], fl[k][:, r * 1024:(r + 1) * 1024], xf[tc][r * 512 + tq * 128:r * 512 + (tq + 1) * 128, :],
                              reads=[xfB[tc]], writes=[flB[k]])
                    P.dma("sp", sm[f"o{k}"], ow[k], xo[tc][tq * 128:(tq + 1) * 128, :], writes=[owB[k]])
                    c = k * 3
                    P.op("act", ACT(junk, fl[k], AF.Square, accum_out=stat[:, c:c + 1]), reads=[flB[k]], writes=[jB, stB[k]])
                    P.op("act", ACT(stat[:, c + 1:c + 2], stat[:, c:c + 1], AF.Sqrt, scale=1.0 / D, bias=EPS), reads=[stB[k]], writes=[stB[k]])
                    P.op("dve", RCP(stat[:, c + 2:c + 3], stat[:, c + 1:c + 2]), reads=[stB[k]], writes=[stB[k]])
                    P.op("dve", STT(ow[k], ow[k], stat[:, c + 2:c + 3], fg, ALU.mult, ALU.mult), reads=[owB[k], stB[k], fgB], writes=[owB[k]])
                    P.dma("sp", sm[f"o{k}"], y_out[tc][tq * 128:(tq + 1) * 128, :], ow[k], reads=[owB[k]])
                P.barrier()

        for l in range(n_layers):
            final = (l == DEPTH - 1)
            if "A" in phases:
                phase_A(l, x_full if l == 0 else xf)
            if "1" in phases:
                phase_DA(l)
            if "2" in phases:
                phase_SB(l)
            if "3" in phases:
                phase_pool(l)
            if "4" in phases:
                phase_mem(l)
            if "C" in phases:
                phase_C(l, x_own if l == 0 else xo, final)
        P.barrier(final=True)
        counts = P.emit(nc, st)
    return nc, counts


def host_constants():
    bf = ml_dtypes.bfloat16
    k = np.arange(128)
    c = {}
    c["c_ident"] = np.eye(128, dtype=np.float32).astype(bf)
    c["c_J"] = np.eye(128, dtype=np.float32)[::-1].copy().astype(bf)
    c["c_trineg"] = (-(k[:, None] >= k[None, :]).astype(np.float32)).astype(bf)
    c["c_ones"] = np.ones((128, 128), np.float32).astype(bf)
    c["c_zeros"] = np.zeros((128, 128), np.float32).astype(bf)
    mt = (k[:, None] < k[None, :]).astype(np.float32)
    c["c_mtri"] = mt.astype(bf)
    c["c_negm"] = (NEG * (1.0 - mt)).astype(np.float32)
    c["c_onesf"] = np.full((128, 128), 1.0 / 128.0, np.float32)
    c["c_ones1f"] = np.ones((128, 128), np.float32)
    n = np.arange(GL) - 511
    oh = np.zeros((33, GL), np.float32)
    bk = t5_bucket_np(n)
    for i in range(GL):
        if n[i] >= 0:
            oh[bk[i], i] = 1.0
        else:
            oh[32, i] = 1.0
    c["c_oh"] = oh
    invc = np.zeros((128, 4, 16), np.float32)
    t = np.arange(16)
    for g, w in enumerate((2, 4, 8, 16)):
        invc[:, g, :] = 1.0 / np.minimum(t + 1, w).astype(np.float32)
    c["c_invc"] = invc
    return c


def _in_cols(r):
    cols = []
    for i in range(8):
        cols.append(i * 1024 + r * 512 + np.arange(512))
    cols.append(8 * 1024 + np.arange(1024))
    for g in range(4):
        cols.append(9 * 1024 + (2 * g + r) * 128 + np.arange(128))
    cols.append(10 * 1024 + r * 512 + np.arange(512))
    cols.append(11 * 1024 + r * 512 + np.arange(512))
    for n in range(4):
        cols.append(12288 + n * 2048 + r * 1024 + np.arange(1024))
    return np.concatenate(cols)


def _branch_rows(r_unused):
    rows = []
    for n in range(4):
        blk = []
        for k in range(2):
            for rr in range(2):
                for i in range(2):
                    loc = 2 * k + i
                    blk.append(2 * loc + rr if n == 2 else 4 * rr + loc)
        rows.append(np.concatenate([b * 128 + np.arange(128) for b in blk]))
    return rows


def host_shared(inputs, r):
    f = lambda a: np.asarray(a, dtype=np.float32)
    m = {}
    m["w_in"] = np.ascontiguousarray(f(inputs["w_in"])[:, :, _in_cols(r)])
    wkv = f(inputs["w_mem_kv"])
    m["w_mem_kv"] = np.ascontiguousarray(np.concatenate([wkv[:, :, r * 512:(r + 1) * 512], wkv[:, :, 1024 + r * 512:1024 + (r + 1) * 512]], axis=2))
    wb = f(inputs["w_branch"])
    rows = _branch_rows(r)
    m["w_branch"] = np.ascontiguousarray(np.stack([wb[:, n][:, rows[n]][:, :, r * 1024:(r + 1) * 1024] for n in range(4)], axis=1))
    m["w_out"] = np.ascontiguousarray(f(inputs["w_out"])[:, :, r * 1024:(r + 1) * 1024])
    m["w_pool"] = np.ascontiguousarray(f(inputs["w_pool"])[:, :, :, r * 128:(r + 1) * 128])
    bc = lambda v: np.ascontiguousarray(np.broadcast_to(f(v)[..., None, :], v.shape[:-1] + (128, v.shape[-1])))
    m["norm_g_bc"] = bc(inputs["norm_g"])
    m["mem_norm_g_bc"] = bc(inputs["mem_norm_g"])
    m["final_g_bc"] = bc(f(inputs["final_g"])[r * 1024:(r + 1) * 1024])
    gbt = f(inputs["gate_b"]).reshape(DEPTH, 4, 2, 8, 128)[:, :, r]
    m["gate_b_t"] = np.ascontiguousarray(gbt.transpose(0, 3, 1, 2).reshape(DEPTH, 128, 32))
    m["da_g_t"] = np.ascontiguousarray(f(inputs["da_norm_g"]).reshape(DEPTH, 8, 128)[:, r * 4:(r + 1) * 4].transpose(0, 2, 1))
    m["pool_scale_t"] = np.ascontiguousarray(f(inputs["pool_scale"]).reshape(DEPTH, 4, 2, 128)[:, :, r].transpose(0, 2, 1))
    lam = np.stack([f(inputs[k]) for k in ("lam_q1", "lam_k1", "lam_q2", "lam_k2")], axis=1)
    m["lam_bc"] = np.ascontiguousarray(np.broadcast_to(lam[:, None], (DEPTH, 128, 4, 64)))
    rb = f(inputs["rel_bias"])[:, r * 4:(r + 1) * 4]
    m["rel_bias"] = np.ascontiguousarray(rb)
    m["rb31_bc"] = np.ascontiguousarray(np.broadcast_to(rb[31][None], (128, 4)))
    return m


def host_acts(inputs, b, r):
    f = lambda a: np.asarray(a, dtype=np.float32)
    x = f(inputs["x"][b])
    m = {}
    m["x_full"] = np.ascontiguousarray(x.reshape(8, 512, 2, 1024).transpose(0, 2, 1, 3).reshape(8, 1024, 1024))
    m["x_own"] = np.ascontiguousarray(x[:, r * 1024:(r + 1) * 1024].reshape(8, 512, 1024))
    m["mem"] = np.ascontiguousarray(f(inputs["mem"][b]))
    return m


_NC_CACHE = {}


def kernel(**inputs):
    if "nc" not in _NC_CACHE:
        _NC_CACHE["nc"] = build_program()[0]
    nc = _NC_CACHE["nc"]
    consts = host_constants()
    shared = [host_shared(inputs, r) for r in range(2)]
    in_maps = []
    for core in range(N_CORES):
        b, r = core // 2, core % 2
        m = dict(shared[r])
        m.update(host_acts(inputs, b, r))
        m.update(consts)
        in_maps.append(m)
    res = run_bass_kernel_spmd(nc, in_maps, core_ids=list(range(N_CORES)))
    out = np.empty((4, S, D), np.float32)
    for core in range(N_CORES):
        b, r = core // 2, core % 2
        out[b][:, r * 1024:(r + 1) * 1024] = np.asarray(res.results[core]["y"], dtype=np.float32).reshape(S, 1024)
    return out
```

```python
import math
from contextlib import ExitStack

import numpy as np
import ml_dtypes

import concourse.bass as bass
import concourse.mybir as mybir
from concourse.bass_utils import run_bass_kernel_spmd

F32 = mybir.dt.float32
BF16 = mybir.dt.bfloat16
U8 = mybir.dt.uint8
AF = mybir.ActivationFunctionType
ALU = mybir.AluOpType
AX = mybir.AxisListType

S = 4096
D = 2048
W = 1024
DEPTH = 2
NCB = 160
EPS = 1e-6
NEG = -30000.0
GL = 1152
ENG = ("pe", "act", "dve", "pool", "sp")
N_CORES = 8
PAIRS = [[0, 1], [2, 3], [4, 5], [6, 7]]
NPB = 84


class Buf:
    __slots__ = ("name", "w", "r")

    def __init__(self, name=""):
        self.name = name
        self.w = {}
        self.r = {}


class DmaSem:
    __slots__ = ("key", "count")

    def __init__(self, key):
        self.key = key
        self.count = 0


class Prog:
    def __init__(self):
        self.ops = {e: [] for e in ENG}
        self.cnt = dict.fromkeys(ENG, 0)
        self.known = {e: {} for e in ENG}
        self.dsems = []
        self.csems = []
        self.swsems = []
        self.sem_i = 0
        self.swsem_i = 0

    def dma_sem(self, sw=False):
        if sw:
            if self.swsem_i == len(self.swsems):
                self.swsems.append(DmaSem("w_%d" % self.swsem_i))
            s = self.swsems[self.swsem_i]
            self.swsem_i += 1
            return s
        if self.sem_i == len(self.dsems):
            self.dsems.append(DmaSem("d_%d" % self.sem_i))
        s = self.dsems[self.sem_i]
        self.sem_i += 1
        return s

    def _waits(self, eng, reads, writes):
        deps = {}
        for b in reads:
            for k, v in b.w.items():
                if deps.get(k, 0) < v:
                    deps[k] = v
        for b in writes:
            for k, v in b.w.items():
                if deps.get(k, 0) < v:
                    deps[k] = v
            for k, v in b.r.items():
                if deps.get(k, 0) < v:
                    deps[k] = v
        kn = self.known[eng]
        waits = []
        for k, v in deps.items():
            if eng == "pe" and k == "pe":
                continue
            if kn.get(k, 0) < v:
                kn[k] = v
                waits.append((k, v))
        return waits

    def op(self, eng, fns, reads=(), writes=()):
        if not isinstance(fns, (list, tuple)):
            fns = [fns]
        waits = self._waits(eng, reads, writes)
        self.cnt[eng] += 1
        v = self.cnt[eng]
        for b in reads:
            b.r[eng] = v
        for b in writes:
            b.w = {eng: v}
            b.r = {}
        n = len(fns)
        for i, fn in enumerate(fns):
            self.ops[eng].append((waits if i == 0 else (), fn, (eng, 1) if i == n - 1 else None))

    def dma(self, q, sem, out, in_, reads=(), writes=()):
        assert (q == "pool") == sem.key.startswith("w_"), (q, sem.key)
        waits = self._waits(q, reads, writes)
        sem.count += 16
        v = sem.count
        for b in reads:
            b.r[sem.key] = v
        for b in writes:
            b.w = {sem.key: v}
            b.r = {}
        self.ops[q].append((waits, lambda e, o=out, i=in_: e.dma_start(out=o, in_=i), (sem.key, 16)))

    def coll_sem(self):
        s = DmaSem("c_%d" % len(self.csems))
        self.csems.append(s)
        return s

    def coll(self, sem, kind, groups, out, in_, reads=(), writes=()):
        waits = self._waits("pool", reads, writes)
        sem.count += 1
        v = sem.count
        for b in reads:
            b.r[sem.key] = v
        for b in writes:
            b.w = {sem.key: v}
            b.r = {}
        self.ops["pool"].append((waits, lambda e, o=out, i=in_: e.collective_compute(
            kind, ALU.bypass, replica_groups=groups, ins=[i], outs=[o]), (sem.key, 1)))

    def barrier(self, final=False):
        snap = {e: self.cnt[e] for e in ENG if self.cnt[e] > 0}
        for s in self.dsems + self.swsems + (self.csems if final else []):
            if s.count > 0:
                snap[s.key] = s.count
        for e in ENG:
            kn = self.known[e]
            waits = []
            for k, v in snap.items():
                if kn.get(k, 0) < v:
                    kn[k] = v
                    waits.append((k, v))
            if waits:
                self.ops[e].append((waits, None, None))
        self.sem_i = 0
        self.swsem_i = 0

    def emit(self, nc, stack):
        handles = {}
        for e in ENG:
            handles[e] = stack.enter_context(nc.semaphore("s_" + e))
        for s in self.dsems + self.swsems + self.csems:
            handles[s.key] = stack.enter_context(nc.semaphore(s.key))
        block = stack.enter_context(nc.Block())
        ops = self.ops

        def run(engine, name):
            for waits, fn, inc in ops[name]:
                for k, v in waits:
                    engine.wait_ge(handles[k], v)
                if fn is not None:
                    ins = fn(engine)
                    if inc is not None:
                        ins.then_inc(handles[inc[0]], inc[1])

        block.tensor(lambda e: run(e, "pe"))
        block.scalar(lambda e: run(e, "act"))
        block.vector(lambda e: run(e, "dve"))
        block.gpsimd(lambda e: run(e, "pool"))
        block.sync(lambda e: run(e, "sp"))
        return {k: len(v) for k, v in ops.items()}


class Ring:
    def __init__(self, items):
        self.items = items
        self.i = 0

    def next(self):
        it = self.items[self.i % len(self.items)]
        self.i += 1
        return it


class Arena:
    def __init__(self, ap, size):
        self.ap = ap
        self.size = size
        self.off = 0

    def alloc(self, free_shape, dt):
        esz = 2 if dt == BF16 else 4
        n = int(np.prod(free_shape)) * esz
        n_al = (n + 63) // 64 * 64
        assert self.off + n_al <= self.size, f"arena overflow {self.off}+{n_al}>{self.size}"
        v = self.ap[:, self.off:self.off + n].bitcast(dt)
        self.off += n_al
        if len(free_shape) == 2:
            v = v.rearrange("p (a b) -> p a b", b=free_shape[1])
        elif len(free_shape) == 3:
            v = v.rearrange("p (a b c) -> p a b c", b=free_shape[1], c=free_shape[2])
        return v


def MM(out, lhsT, rhs, start, stop):
    return lambda e: e.matmul(out, lhsT, rhs, start=start, stop=stop)


def TR(out, in_, ident):
    return lambda e: e.transpose(out, in_, ident)


def ACT(out, in_, func, **kw):
    return lambda e: e.activation(out=out, in_=in_, func=func, **kw)


def TT(out, in0, in1, op):
    return lambda e: e.tensor_tensor(out=out, in0=in0, in1=in1, op=op)


def STT(out, in0, scalar, in1, op0, op1):
    return lambda e: e.scalar_tensor_tensor(out=out, in0=in0, scalar=scalar, in1=in1, op0=op0, op1=op1)


def TS(out, in0, s1, s2, op0, op1):
    return lambda e: e.tensor_scalar(out=out, in0=in0, scalar1=s1, scalar2=s2, op0=op0, op1=op1)


def TSM(out, in0, s1):
    return lambda e: e.tensor_scalar_mul(out=out, in0=in0, scalar1=s1)


def CP(out, in_):
    return lambda e: e.tensor_copy(out=out, in_=in_)


def RCP(out, in_):
    return lambda e: e.reciprocal(out=out, in_=in_)


def MSET(out, val):
    return lambda e: e.memset(out, val)


def t5_bucket_np(n):
    n = np.maximum(n, 0)
    nf = np.maximum(n, 1).astype(np.float32)
    large = 16 + (np.log(nf / np.float32(16)) / np.float32(math.log(128 / 16)) * np.float32(16)).astype(np.int32)
    large = np.minimum(large, 31)
    return np.where(n < 16, n, large)


DAQ, DAK, DAV, DAZ, SBQ, SBK, SBV, SBZ, PU, PZ, MQ, MZ, GT = 0, 4, 8, 12, 16, 20, 24, 28, 32, 40, 44, 48, 52
SILU_BLOCKS = set(range(12, 16)) | set(range(28, 32)) | set(range(40, 44)) | set(range(48, 52))


def build_program(n_layers=DEPTH, phases="A1234C", debug=False):
    nc = bass.Bass("TRN2", target_bir_lowering=False)
    P = Prog()

    def din(name, shape, dt=F32):
        return nc.dram_tensor(name, shape, dt, kind="ExternalInput").ap()

    x_full = din("x_full", [8, 1024, 1024])
    x_own = din("x_own", [8, 512, 1024])
    mem_in = din("mem", [256, D])
    w_in = din("w_in", [DEPTH, D, NPB * 128])
    w_mkv = din("w_mem_kv", [DEPTH, D, 1024])
    w_br = din("w_branch", [DEPTH, 4, W, 1024])
    w_out = din("w_out", [DEPTH, D, 1024])
    w_pool = din("w_pool", [DEPTH, 4, 256, 128])
    norm_g_bc = din("norm_g_bc", [DEPTH, 128, D])
    memg_bc = din("mem_norm_g_bc", [DEPTH, 128, D])
    fing_bc = din("final_g_bc", [128, 1024])
    gate_b_t = din("gate_b_t", [DEPTH, 128, 32])
    da_g_t = din("da_g_t", [DEPTH, 128, 4])
    psc_t = din("pool_scale_t", [DEPTH, 128, 4])
    lam_bc = din("lam_bc", [DEPTH, 128, 4, 64])
    rel_bias = din("rel_bias", [32, 4])
    rb31_bc = din("rb31_bc", [128, 4])
    c_ident = din("c_ident", [128, 128], BF16)
    c_J = din("c_J", [128, 128], BF16)
    c_trineg = din("c_trineg", [128, 128], BF16)
    c_ones = din("c_ones", [128, 128], BF16)
    c_zeros = din("c_zeros", [128, 128], BF16)
    c_mtri = din("c_mtri", [128, 128], BF16)
    c_negm = din("c_negm", [128, 128], F32)
    c_onesf = din("c_onesf", [128, 128], F32)
    c_ones1f = din("c_ones1f", [128, 128], F32)
    c_oh = din("c_oh", [33, GL], F32)
    c_invc = din("c_invc", [128, 4, 16], F32)

    y_out = nc.dram_tensor("y", [8, 512, 1024], F32, kind="ExternalOutput").ap()
    skind = dict(kind="ExternalOutput") if debug else {}
    proj = nc.dram_tensor("proj", [NPB, 128, S], BF16, **skind).ap()
    br_loc = nc.dram_tensor("br_loc", [16 * 128, S], BF16, **skind).ap()
    br_all = nc.dram_tensor("br_all", [8, 512, S], BF16).ap()
    mg_loc = nc.dram_tensor("mg_loc", [4, 1024, 1024], BF16).ap()
    mg_all = nc.dram_tensor("mg_all", [4, 2048, 1024], BF16).ap()
    xo = nc.dram_tensor("xo", [8, 512, 1024], F32, **skind).ap()
    xf = nc.dram_tensor("xf", [8, 1024, 1024], F32).ap()
    gs_t = nc.dram_tensor("gs", [4, GL], BF16)

    with ExitStack() as st:
        ARENA_BYTES = 200 * 1024
        arena_t = st.enter_context(nc.sbuf_tensor("arena", [128, ARENA_BYTES], U8))
        ps = st.enter_context(nc.psum_tensor("ps", [128, 8, 512], F32))
        A = Arena(arena_t, ARENA_BYTES)
        bank = [ps[:, i, :] for i in range(8)]
        bankB = [Buf(f"bank{i}") for i in range(8)]
        psb = ps[:, 7, :].bitcast(BF16)
        psb6 = ps[:, 6, :].bitcast(BF16)

        brallB = [Buf(f"brall{i}") for i in range(8)]
        mgallB = [Buf(f"mgall{i}") for i in range(4)]
        xfB = [Buf(f"xf{i}") for i in range(8)]
        brS = [P.coll_sem() for _ in range(8)]
        mgS = [P.coll_sem() for _ in range(4)]
        xfS = [P.coll_sem() for _ in range(8)]

        ident = A.alloc([128], BF16)
        Jm = A.alloc([128], BF16)
        trineg = A.alloc([128], BF16)
        ones_bf = A.alloc([128], BF16)
        zeros_bf = A.alloc([128], BF16)
        mtri = A.alloc([128], BF16)
        negm = A.alloc([128], F32)
        ones_f = A.alloc([128], F32)
        ones1_f = A.alloc([128], F32)
        rb31 = A.alloc([4], F32)
        invc = A.alloc([4, 16], F32)
        cs = P.dma_sem()
        for dst, src in ((ident, c_ident), (Jm, c_J), (trineg, c_trineg), (ones_bf, c_ones), (zeros_bf, c_zeros),
                         (mtri, c_mtri), (negm, c_negm), (ones_f, c_onesf), (ones1_f, c_ones1f), (rb31, rb31_bc), (invc, c_invc)):
            P.dma("sp", cs, dst, src)
        P.barrier()
        base_mark = A.off

        if "1" in phases:
            lhs = A.alloc([4], F32)
            oh = A.alloc([GL], F32)
            gsb = A.alloc([GL], BF16)
            tB = Buf("t5")
            ts_ = P.dma_sem()
            P.dma("sp", ts_, lhs[0:32, :], rel_bias, writes=[tB])
            P.dma("sp", ts_, oh[0:33, :], c_oh, writes=[tB])
            P.op("dve", TSM(lhs[0:32, :], lhs[0:32, :], 8.0), reads=[tB], writes=[tB])
            P.op("dve", MSET(lhs[32:33, :], NEG), writes=[tB])
            for i, (c0, c1) in enumerate(((0, 512), (512, 1024), (1024, GL))):
                P.op("pe", MM(bank[i][0:4, 0:c1 - c0], lhs[0:33, :], oh[0:33, c0:c1], True, True), reads=[tB], writes=[bankB[i]])
                P.op("dve", CP(gsb[0:4, c0:c1], bank[i][0:4, 0:c1 - c0]), reads=[bankB[i]], writes=[tB])
            P.dma("sp", ts_, gs_t.ap(), gsb[0:4, :], reads=[tB])
            P.barrier()
            A.off = base_mark

        def phase_A(l, xsrc):
            A.off = base_mark
            hT = A.alloc([16, 2048], BF16)
            wsl = [A.alloc([16, 512], BF16) for _ in range(3)]
            stg = [A.alloc([2048], F32) for _ in range(3)]
            hbf = [A.alloc([2048], BF16) for _ in range(2)]
            gbc = A.alloc([2048], F32)
            gb = A.alloc([32], F32)
            stat = A.alloc([64], F32)
            hTB = [Buf(f"hT{i}") for i in range(4)]
            wslB = [Buf() for _ in range(3)]
            wslS = [P.dma_sem(sw=True) for i in range(3)]
            stgB = [[Buf() for _ in range(4)] for _ in range(3)]
            stgS = [P.dma_sem() for i in range(3)]
            hbfB = [Buf() for _ in range(2)]
            statB = [Buf() for _ in range(16)]
            gB = Buf()
            ms = P.dma_sem()
            P.dma("sp", ms, gbc, norm_g_bc[l], writes=[gB])
            P.dma("sp", ms, gb, gate_b_t[l], writes=[gB])
            w_l = w_in[l].rearrange("(c p) n -> p c n", p=128)
            pring = Ring([0, 1, 2, 3, 4, 5])
            stg_i = 0
            for hf in range(2):
                for tb in range(16):
                    gtb = hf * 16 + tb
                    tc, tq = gtb // 4, gtb % 4
                    s = stg_i % 3
                    stg_i += 1
                    hs = tb % 2
                    sB = stgB[s]
                    for r in range(2):
                        P.dma("sp", stgS[s], stg[s][:, r * 1024:(r + 1) * 1024],
                              xsrc[tc][r * 512 + tq * 128:r * 512 + (tq + 1) * 128, :],
                              reads=[xfB[tc]] if l > 0 else [], writes=sB)
                    c = tb * 3
                    P.op("act", ACT(hbf[hs], stg[s], AF.Square, accum_out=stat[:, c:c + 1]),
                         reads=sB, writes=[hbfB[hs], statB[tb]])
                    P.op("act", ACT(stat[:, c + 1:c + 2], stat[:, c:c + 1], AF.Sqrt, scale=1.0 / D, bias=EPS),
                         reads=[statB[tb]], writes=[statB[tb]])
                    P.op("dve", RCP(stat[:, c + 2:c + 3], stat[:, c + 1:c + 2]), reads=[statB[tb]], writes=[statB[tb]])
                    P.op("dve", STT(hbf[hs], stg[s], stat[:, c + 2:c + 3], gbc, ALU.mult, ALU.mult),
                         reads=sB + [statB[tb], gB], writes=[hbfB[hs]])
                    for g in range(2):
                        pb, pB = (psb6, bankB[6]) if g == 0 else (psb, bankB[7])
                        P.op("pe", [TR(pb[:, i * 128:(i + 1) * 128], hbf[hs][:, (g * 8 + i) * 128:(g * 8 + i + 1) * 128], ident)
                                    for i in range(8)], reads=[hbfB[hs]], writes=[pB])
                        src = pb.rearrange("p (a b) -> p a b", b=128)
                        dst = hT[:, g * 8:(g + 1) * 8, tb * 128:(tb + 1) * 128]
                        if g == 0:
                            P.op("act", ACT(dst, src, AF.Copy), reads=[pB], writes=[hTB[tb // 4]])
                        else:
                            P.op("dve", CP(dst, src), reads=[pB], writes=[hTB[tb // 4]])
                for ws in range(NPB // 4):
                    sl = ws % 3
                    P.dma("pool", wslS[sl], wsl[sl], w_l[:, :, ws * 512:(ws + 1) * 512], writes=[wslB[sl]])
                    for j in range(4):
                        cb = ws * 4 + j
                        s = stg_i % 3
                        stg_i += 1
                        ostg = stg[s].bitcast(BF16)
                        for tcl in range(4):
                            bk = pring.next()
                            P.op("pe", [MM(bank[bk], wsl[sl][:, c, j * 128:(j + 1) * 128], hT[:, c, tcl * 512:(tcl + 1) * 512],
                                           c == 0, c == 15) for c in range(16)],
                                 reads=[wslB[sl], hTB[tcl]], writes=[bankB[bk]])
                            dst = ostg[:, tcl * 512:(tcl + 1) * 512]
                            oB = [stgB[s][tcl]]
                            if cb >= GT:
                                P.op("act", ACT(dst, bank[bk], AF.Sigmoid, bias=gb[:, cb - GT:cb - GT + 1]),
                                     reads=[bankB[bk], gB], writes=oB)
                            elif cb in SILU_BLOCKS:
                                P.op("act", ACT(dst, bank[bk], AF.Silu), reads=[bankB[bk]], writes=oB)
                            elif SBQ <= cb < SBQ + 4:
                                P.op("dve", TSM(dst, bank[bk], 128.0 ** -0.5), reads=[bankB[bk]], writes=oB)
                            elif MQ <= cb < MQ + 4:
                                P.op("dve", TSM(dst, bank[bk], 1.0 / 16.0), reads=[bankB[bk]], writes=oB)
                            else:
                                P.op("dve", CP(dst, bank[bk]), reads=[bankB[bk]], writes=oB)
                        P.dma("sp", stgS[s], proj[cb][:, hf * 2048:(hf + 1) * 2048], ostg[:, 0:2048], reads=stgB[s])
            P.barrier()

        def gather_branch(n):
            for k in range(2):
                i = n * 2 + k
                r0 = (n * 4 + 2 * k) * 128
                P.coll(brS[i], "AllGather", PAIRS, br_all[i], br_loc[r0:r0 + 256, :], writes=[brallB[i]])

        def attn_common_alloc():
            A.off = base_mark
            d = {}
            for nme in ("qT", "kT", "vT", "zT", "ostg"):
                d[nme] = [A.alloc([S], BF16) for _ in range(2)]
            d["vtok"] = [A.alloc([32, 128], BF16) for _ in range(2)]
            return d

        def load_head(d, hs, cbs, hB, hS, transposes=True, ring=None):
            for name, cb in zip(("qT", "kT", "vT", "zT"), cbs):
                P.dma("sp", hS[hs][name], d[name][hs], proj[cb], writes=[hB[hs][name]])
            if transposes:
                head_transposes(d, hs, hB, ring)

        def head_transposes(d, hs, hB, ring=None):
            for g in range(4):
                bk = 7 if ring is None else ring.next()
                pb = ps[:, bk, :].bitcast(BF16)
                P.op("pe", [TR(pb[:, i * 128:(i + 1) * 128], d["vT"][hs][:, (g * 8 + i) * 128:(g * 8 + i + 1) * 128], ident)
                            for i in range(8)], reads=[hB[hs]["vT"]], writes=[bankB[bk]])
                P.op("dve", CP(d["vtok"][hs][:, g * 8:(g + 1) * 8, :], pb.rearrange("p (a b) -> p a b", b=128)),
                     reads=[bankB[bk]], writes=[hB[hs]["vtok"]])

        NH = 4

        def phase_DA(l):
            d = attn_common_alloc()
            lam_init = 0.8 - 0.6 * math.exp(-0.3 * l)
            Tp = A.alloc([NH, 1024], BF16)
            Er = [A.alloc([512], BF16) for _ in range(12)]
            tmp = [[A.alloc([512], F32) for _ in range(3)] for _ in range(2)]
            lamt = A.alloc([4, 64], F32)
            lsm = A.alloc([8], F32)
            gsc = A.alloc([NH], F32)
            names = ("qT", "kT", "vT", "zT", "vtok", "ostg")
            hB = [{n: Buf(n) for n in names} for _ in range(2)]
            hS = [{n: P.dma_sem() for n in names} for i in range(2)]
            ErB = [Buf() for _ in range(12)]
            tmpB = [[Buf() for _ in range(3)] for _ in range(2)]
            sB = Buf()
            ss = P.dma_sem()
            P.dma("sp", ss, lamt, lam_bc[l], writes=[sB])
            P.dma("sp", ss, gsc, da_g_t[l], writes=[sB])
            for h in range(NH):
                P.dma("sp", ss, Tp[:, h, :], bass.AP(gs_t, h * GL, [[1, 128], [1, 1024]]), writes=[sB])
            P.op("dve", TT(lamt[:, 0, :], lamt[:, 0, :], lamt[:, 1, :], ALU.mult), reads=[sB], writes=[sB])
            P.op("dve", TT(lamt[:, 2, :], lamt[:, 2, :], lamt[:, 3, :], ALU.mult), reads=[sB], writes=[sB])
            P.op("dve", lambda e: e.reduce_sum(out=lsm[:, 0:1], in_=lamt[:, 0, :], axis=AX.X), reads=[sB], writes=[sB])
            P.op("dve", lambda e: e.reduce_sum(out=lsm[:, 1:2], in_=lamt[:, 2, :], axis=AX.X), reads=[sB], writes=[sB])
            P.op("act", ACT(lsm[:, 2:4], lsm[:, 0:2], AF.Exp), reads=[sB], writes=[sB])
            P.op("dve", TT(lsm[:, 4:5], lsm[:, 3:4], lsm[:, 2:3], ALU.subtract), reads=[sB], writes=[sB])
            P.op("dve", lambda e: e.tensor_scalar_add(out=lsm[:, 5:6], in0=lsm[:, 4:5], scalar1=-lam_init), reads=[sB], writes=[sB])
            P.op("dve", TSM(gsc, gsc, 1.0 - lam_init), reads=[sB], writes=[sB])
            neglam = lsm[:, 5:6]

            Zc = [[A.alloc([512], F32) for _ in range(3)] for _ in range(2)]
            ZcB = [[Buf() for _ in range(3)] for _ in range(2)]
            acc_sets = ((0, 1), (2, 3))
            sring = Ring([4, 5, 6])
            ering = Ring(list(range(12)))
            load_head(d, 0, (DAQ, DAK, DAV, DAZ), hB, hS, ring=sring)
            for h in range(NH):
                hs = h % 2
                if h + 1 < NH:
                    load_head(d, (h + 1) % 2, (DAQ + h + 1, DAK + h + 1, DAV + h + 1, DAZ + h + 1), hB, hS, transposes=False)
                qT, kT, vtok, zT, ostg = d["qT"][hs], d["kT"][hs], d["vtok"][hs], d["zT"][hs], d["ostg"][hs]
                B = hB[hs]
                tiles = [(qc, kb) for qc in range(8) for kb in range(4 * qc + 4)]
                state = {}
                pending = []

                def stage0(qc, kb):
                    j = kb - 4 * qc
                    qs = max(j, 0) * 128
                    near = j >= -1
                    es = []
                    for c in range(2):
                        bk = sring.next()
                        fns = [MM(bank[bk][:, qs:512], kT[c * 64:(c + 1) * 64, kb * 128:(kb + 1) * 128],
                                  qT[c * 64:(c + 1) * 64, qc * 512 + qs:(qc + 1) * 512], True, not near)]
                        if near:
                            off = 128 * (3 - j)
                            fns.append(MM(bank[bk][:, qs:512], Jm, Tp[:, h, off + qs:off + 512], False, True))
                        P.op("pe", fns, reads=[B["kT"], B["qT"], sB], writes=[bankB[bk]])
                        ei = ering.next()
                        bias = 0.0 if near else rb31[:, h:h + 1]
                        P.op("act", ACT(Er[ei][:, qs:512], bank[bk][:, qs:512], AF.Exp, scale=0.125, bias=bias),
                             reads=[bankB[bk]], writes=[ErB[ei]])
                        es.append(ei)
                    state[(qc, kb)] = (qs, es)

                def stage1(qc, kb):
                    qs, es = state.pop((qc, kb))
                    first = kb == 0
                    last = kb == 4 * qc + 3
                    k = qc % 2
                    acc_o = acc_sets[k]
                    if first:
                        P.op("dve", MSET(Zc[k][1], 0.0), writes=[ZcB[k][1]])
                    for c in range(2):
                        E = Er[es[c]]
                        if c == 0:
                            E1 = Er[es[1]]
                            P.op("pe", [MM(bank[acc_o[0]][:, qs:512], vtok[:, kb, :], E[:, qs:512], first, last),
                                        MM(bank[acc_o[1]][:, qs:512], vtok[:, kb, :], E1[:, qs:512], first, last),
                                        MM(bank[7][:, qs:512], ones_bf, E[:, qs:512], first, last)],
                                 reads=[B["vtok"], ErB[es[0]], ErB[es[1]]], writes=[bankB[acc_o[0]], bankB[acc_o[1]], bankB[7]])
                            continue
                        if kb % 2 == 0:
                            eng, zi = "pool", 2
                        else:
                            eng, zi = "dve", 1
                        if first:
                            P.op(eng, CP(Zc[k][zi], E), reads=[ErB[es[c]]], writes=[ZcB[k][zi]])
                        else:
                            P.op(eng, TT(Zc[k][zi][:, qs:512], Zc[k][zi][:, qs:512], E[:, qs:512], ALU.add),
                                 reads=[ErB[es[c]], ZcB[k][zi]], writes=[ZcB[k][zi]])
                    if last:
                        t0_, b0_ = tmp[k][0], tmpB[k][0]
                        P.op("dve", RCP(t0_, bank[7]), reads=[bankB[7]], writes=[b0_])
                        P.op("dve", TT(t0_, bank[acc_o[0]], t0_, ALU.mult), reads=[bankB[acc_o[0]], b0_], writes=[b0_])
                        pending.append([2, lambda qc=qc: combine(qc)])

                def combine(qc):
                    k = qc % 2
                    acc_o = acc_sets[k]
                    t0, t1, t2 = tmp[k]
                    b0, b1, b2 = tmpB[k]
                    z1 = sring.next()
                    P.op("pe", [MM(bank[z1], ones1_f, Zc[k][1], True, False), MM(bank[z1], ones1_f, Zc[k][2], False, True)],
                         reads=[ZcB[k][1], ZcB[k][2]], writes=[bankB[z1]])
                    P.op("dve", RCP(t1, bank[z1]), reads=[bankB[z1]], writes=[b1])
                    P.op("dve", TT(t1, bank[acc_o[1]], t1, ALU.mult), reads=[bankB[acc_o[1]], b1], writes=[b1])
                    P.op("dve", STT(t0, t1, neglam, t0, ALU.mult, ALU.add), reads=[b0, b1, sB], writes=[b0])
                    P.op("dve", TT(t2, t0, t0, ALU.mult), reads=[b0], writes=[b2])

                    def part2():
                        bk = sring.next()
                        P.op("pe", MM(bank[bk], ones_f, t2, True, True), reads=[b2], writes=[bankB[bk]])
                        P.op("act", ACT(t2, bank[bk], AF.Ln, bias=EPS), reads=[bankB[bk]], writes=[b2])
                        P.op("act", ACT(t2, t2, AF.Exp, scale=-0.5), reads=[b2], writes=[b2])
                        P.op("dve", TT(t0, t0, t2, ALU.mult), reads=[b0, b2], writes=[b0])
                        P.op("dve", STT(ostg[:, qc * 512:(qc + 1) * 512], t0, gsc[:, h:h + 1], zT[:, qc * 512:(qc + 1) * 512],
                                        ALU.mult, ALU.mult), reads=[b0, sB, B["zT"]], writes=[B["ostg"]])
                    pending.append([3, part2])

                n = len(tiles)
                for s_ in range(n + 1):
                    if s_ < n:
                        stage0(*tiles[s_])
                    if s_ >= 1:
                        stage1(*tiles[s_ - 1])
                    if s_ == n // 2 and h + 1 < NH:
                        head_transposes(d, (h + 1) % 2, hB, sring)
                    for it in pending:
                        it[0] -= 1
                    for it in [it for it in pending if it[0] <= 0]:
                        pending.remove(it)
                        it[1]()
                while pending:
                    it = pending.pop(0)
                    it[1]()
                P.dma("sp", hS[hs]["ostg"], br_loc[(0 + h) * 128:(1 + h) * 128, :], ostg, reads=[B["ostg"]])
            P.barrier()
            gather_branch(0)

        def phase_SB(l):
            d = attn_common_alloc()
            ebuf = [A.alloc([512], F32) for _ in range(2)]
            spb = [A.alloc([512], BF16) for _ in range(4)]
            argb = [A.alloc([512], F32) for _ in range(3)]
            Ab = [A.alloc([512], BF16) for _ in range(5)]
            Rsb = [A.alloc([512], F32) for _ in range(2)]
            names = ("qT", "kT", "vT", "zT", "vtok", "ostg")
            hB = [{n: Buf(n) for n in names} for _ in range(2)]
            hS = [{n: P.dma_sem() for n in names} for i in range(2)]
            ebufB = [Buf() for _ in range(2)]
            spB = [Buf() for _ in range(4)]
            argB = [Buf() for _ in range(3)]
            AbB = [Buf() for _ in range(5)]
            RsB = [Buf() for _ in range(2)]
            zring = Ring([0, 1, 2, 3])
            cring = Ring([4, 5])
            acc = (6, 7)
            e_r, sp_r, arg_r, A_r = Ring([0, 1]), Ring([0, 1, 2, 3]), Ring([0, 1, 2]), Ring([0, 1, 2, 3, 4])
            load_head(d, 0, (SBQ, SBK, SBV, SBZ), hB, hS, ring=cring)
            chunk_ctr = [0]
            for h in range(NH):
                hs = h % 2
                if h + 1 < NH:
                    load_head(d, (h + 1) % 2, (SBQ + h + 1, SBK + h + 1, SBV + h + 1, SBZ + h + 1), hB, hS, transposes=False)
                qT, kT, vtok, zT, ostg = d["qT"][hs], d["kT"][hs], d["vtok"][hs], d["zT"][hs], d["ostg"][hs]
                B = hB[hs]
                tiles = [(qc, kb) for qc in range(8) for kb in reversed(range(4 * qc + 4))]
                state = {}

                def stage0(qc, kb):
                    j = kb - 4 * qc
                    qs = max(j, 0) * 128
                    bk = zring.next()
                    P.op("pe", MM(bank[bk][:, qs:512], kT[:, kb * 128:(kb + 1) * 128], qT[:, qc * 512 + qs:(qc + 1) * 512],
                                  True, True), reads=[B["kT"], B["qT"]], writes=[bankB[bk]])
                    ei, si = e_r.next(), sp_r.next()
                    P.op("act", ACT(ebuf[ei][:, qs:512], bank[bk][:, qs:512], AF.Exp), reads=[bankB[bk]], writes=[ebufB[ei]])
                    P.op("act", ACT(spb[si][:, qs:512], ebuf[ei][:, qs:512], AF.Ln, bias=1.0), reads=[ebufB[ei]], writes=[spB[si]])
                    if j >= 0:
                        P.op("dve", TT(spb[si][:, qs:qs + 128], spb[si][:, qs:qs + 128], mtri, ALU.mult),
                             reads=[spB[si]], writes=[spB[si]])
                    state[(qc, kb)] = dict(qs=qs, bk=bk, si=si, j=j)

                def stage1(qc, kb):
                    stt_ = state[(qc, kb)]
                    qs, bk, si, j = stt_["qs"], stt_["bk"], stt_["si"], stt_["j"]
                    first = kb == 4 * qc + 3
                    if first:
                        chunk_ctr[0] += 1
                    R = Rsb[chunk_ctr[0] % 2]
                    RB = RsB[chunk_ctr[0] % 2]
                    if first:
                        P.op("dve", MSET(R, 0.0), writes=[RB])
                    ck = cring.next()
                    P.op("pe", [MM(bank[bk][:, qs:512], trineg, spb[si][:, qs:512], False, True),
                                MM(bank[ck][:, qs:512], ones_bf, spb[si][:, qs:512], True, True)],
                         reads=[spB[si]], writes=[bankB[bk], bankB[ck]])
                    ai = arg_r.next()
                    P.op("dve", TT(argb[ai][:, qs:512], bank[bk][:, qs:512], R[:, qs:512], ALU.subtract),
                         reads=[bankB[bk], RB], writes=[argB[ai]])
                    if j >= 0:
                        P.op("dve", TT(argb[ai][:, qs:qs + 128], argb[ai][:, qs:qs + 128], negm, ALU.add),
                             reads=[argB[ai]], writes=[argB[ai]])
                    if kb > 0:
                        P.op("dve", TT(R[:, qs:512], R[:, qs:512], bank[ck][:, qs:512], ALU.add), reads=[bankB[ck], RB], writes=[RB])
                    Ai = A_r.next()
                    P.op("act", ACT(Ab[Ai][:, qs:512], argb[ai][:, qs:512], AF.Exp), reads=[argB[ai]], writes=[AbB[Ai]])
                    stt_["Ai"] = Ai

                def stage2(qc, kb):
                    stt_ = state.pop((qc, kb))
                    qs, Ai = stt_["qs"], stt_["Ai"]
                    a = acc[qc % 2]
                    first = kb == 4 * qc + 3
                    last = kb == 0
                    fns = []
                    if first:
                        fns.append(MM(bank[a], zeros_bf, qT[:, 0:512], True, False))
                    fns.append(MM(bank[a][:, qs:512], vtok[:, kb, :], Ab[Ai][:, qs:512], False, last))
                    P.op("pe", fns, reads=[B["vtok"], AbB[Ai], B["qT"]], writes=[bankB[a]])
                    if last:
                        P.op("dve", TT(ostg[:, qc * 512:(qc + 1) * 512], bank[a], zT[:, qc * 512:(qc + 1) * 512], ALU.mult),
                             reads=[bankB[a], B["zT"]], writes=[B["ostg"]])

                n = len(tiles)
                L1, L2 = 2, 4
                for s_ in range(n + L2):
                    if s_ < n:
                        stage0(*tiles[s_])
                    if L1 <= s_ < n + L1:
                        stage1(*tiles[s_ - L1])
                    if s_ >= L2:
                        stage2(*tiles[s_ - L2])
                    if s_ == n // 2 and h + 1 < NH:
                        head_transposes(d, (h + 1) % 2, hB, cring)
                P.dma("sp", hS[hs]["ostg"], br_loc[(4 + h) * 128:(5 + h) * 128, :], ostg, reads=[B["ostg"]])
            P.barrier()
            gather_branch(1)

        def phase_pool(l):
            A.off = base_mark
            PADL = 16
            ubf = [A.alloc([S], BF16) for _ in range(2)]
            pz = A.alloc([S], BF16)
            pooled = [A.alloc([S], BF16) for _ in range(2)]
            X = [A.alloc([PADL + S], F32) for _ in range(3)]
            ostg = A.alloc([S], BF16)
            wp = A.alloc([2, 128], BF16)
            psc = A.alloc([4], F32)
            t16 = A.alloc([16], F32)
            ubB = [Buf() for _ in range(2)]
            pzB = Buf()
            poB = [Buf() for _ in range(2)]
            XB = [Buf() for _ in range(3)]
            osB = Buf()
            wpB, pscB, t16B = Buf(), Buf(), Buf()
            sems = {k: P.dma_sem(sw=(k == "w")) for k in ("u0", "u1", "z", "o", "w", "m")}
            P.dma("sp", sems["m"], psc, psc_t[l], writes=[pscB])
            for i in range(3):
                P.op("dve", MSET(X[i][:, 0:PADL], 0.0), writes=[XB[i]])
            pr = Ring([0, 1, 2, 3, 4, 5])
            for g in range(4):
                wwin = 2 ** (g + 1)
                P.dma("pool", sems["w"], wp, w_pool[l, g].rearrange("(c p) n -> p c n", p=128), writes=[wpB])
                for cb in range(2):
                    P.dma("sp", sems[f"u{cb}"], ubf[cb], proj[PU + 2 * g + cb], writes=[ubB[cb]])
                P.dma("sp", sems["z"], pz, proj[PZ + g], writes=[pzB])
                for cb in range(2):
                    for hh in range(2):
                        sl_ = slice(PADL + hh * 2048, PADL + (hh + 1) * 2048)
                        P.op("dve", CP(X[0][:, sl_], ubf[cb][:, hh * 2048:(hh + 1) * 2048]), reads=[ubB[cb]], writes=[XB[0]])
                    cur = 0
                    for lev in range(g + 1):
                        sh = 2 ** lev
                        nxt = 1 if cur != 1 else 2
                        for hh in range(2):
                            a0 = PADL + hh * 2048
                            P.op("dve", TT(X[nxt][:, a0:a0 + 2048], X[cur][:, a0:a0 + 2048], X[cur][:, a0 - sh:a0 + 2048 - sh], ALU.add),
                                 reads=[XB[cur]], writes=[XB[nxt]])
                        cur = nxt
                    for hh in range(2):
                        a0 = PADL + hh * 2048
                        P.op("dve", STT(pooled[cb][:, hh * 2048:(hh + 1) * 2048], X[cur][:, a0:a0 + 2048], 1.0 / wwin,
                                        X[0][:, a0:a0 + 2048], ALU.mult, ALU.subtract), reads=[XB[cur], XB[0]], writes=[poB[cb]])
                    P.op("dve", TT(t16, X[cur][:, PADL:PADL + 16], invc[:, g, :], ALU.mult), reads=[XB[cur]], writes=[t16B])
                    P.op("dve", TT(pooled[cb][:, 0:16], t16, X[0][:, PADL:PADL + 16], ALU.subtract), reads=[t16B, XB[0]], writes=[poB[cb]])
                for tc in range(8):
                    bk = pr.next()
                    P.op("pe", [MM(bank[bk], wp[:, c, :], pooled[c][:, tc * 512:(tc + 1) * 512], c == 0, c == 1)
                                for c in range(2)], reads=[wpB, poB[0], poB[1]], writes=[bankB[bk]])
                    P.op("dve", STT(ostg[:, tc * 512:(tc + 1) * 512], bank[bk], psc[:, g:g + 1],
                                    pz[:, tc * 512:(tc + 1) * 512], ALU.mult, ALU.mult),
                         reads=[bankB[bk], pscB, pzB], writes=[osB])
                P.dma("sp", sems["o"], br_loc[(8 + g) * 128:(9 + g) * 128, :], ostg, reads=[osB])
            P.barrier()
            gather_branch(2)

        def phase_mem(l):
            A.off = base_mark
            mstg = [A.alloc([D], F32) for _ in range(2)]
            mh = [A.alloc([D], BF16) for _ in range(2)]
            gbc = A.alloc([D], F32)
            stat = A.alloc([8], F32)
            memT = A.alloc([16, 256], BF16)
            wsl = [A.alloc([16, 512], BF16) for _ in range(2)]
            mkT = A.alloc([4, 256], BF16)
            mv = A.alloc([2, 512], BF16)
            qT = [A.alloc([S], BF16) for _ in range(2)]
            mz = [A.alloc([S], BF16) for _ in range(2)]
            ostg = [A.alloc([S], BF16) for _ in range(2)]
            Eb = [A.alloc([512], BF16) for _ in range(2)]
            tm = [A.alloc([512], F32) for _ in range(2)]
            mB = [Buf() for _ in range(2)]
            mhB = [Buf() for _ in range(2)]
            gB, stB, memTB, mkB, mvB = Buf(), Buf(), Buf(), Buf(), Buf()
            wB = [Buf() for _ in range(2)]
            qB = [Buf() for _ in range(2)]
            zB = [Buf() for _ in range(2)]
            oB = [Buf() for _ in range(2)]
            EB = [Buf() for _ in range(2)]
            tB = [Buf() for _ in range(2)]
            sm = {k: P.dma_sem(sw=k.startswith("w")) for k in ("m0", "m1", "g", "w0", "w1", "q0", "q1", "z0", "z1", "o0", "o1")}
            P.dma("sp", sm["g"], gbc, memg_bc[l], writes=[gB])
            for mb in range(2):
                P.dma("sp", sm[f"m{mb}"], mstg[mb], mem_in[mb * 128:(mb + 1) * 128, :], writes=[mB[mb]])
                c = mb * 3
                P.op("act", ACT(mh[mb], mstg[mb], AF.Square, accum_out=stat[:, c:c + 1]), reads=[mB[mb]], writes=[mhB[mb], stB])
                P.op("act", ACT(stat[:, c + 1:c + 2], stat[:, c:c + 1], AF.Sqrt, scale=1.0 / D, bias=EPS), reads=[stB], writes=[stB])
                P.op("dve", RCP(stat[:, c + 2:c + 3], stat[:, c + 1:c + 2]), reads=[stB], writes=[stB])
                P.op("dve", STT(mh[mb], mstg[mb], stat[:, c + 2:c + 3], gbc, ALU.mult, ALU.mult),
                     reads=[mB[mb], stB, gB], writes=[mhB[mb]])
                for g in range(2):
                    P.op("pe", [TR(psb[:, i * 128:(i + 1) * 128], mh[mb][:, (g * 8 + i) * 128:(g * 8 + i + 1) * 128], ident)
                                for i in range(8)], reads=[mhB[mb]], writes=[bankB[7]])
                    P.op("dve", CP(memT[:, g * 8:(g + 1) * 8, mb * 128:(mb + 1) * 128], psb.rearrange("p (a b) -> p a b", b=128)),
                         reads=[bankB[7]], writes=[memTB])
            wv = w_mkv[l].rearrange("(c p) n -> p c n", p=128)
            pr = Ring([0, 1, 2, 3, 4, 5])
            for ws in range(2):
                sl = ws
                P.dma("pool", sm[f"w{sl}"], wsl[sl], wv[:, :, ws * 512:(ws + 1) * 512], writes=[wB[sl]])
                if ws == 0:
                    for j in range(4):
                        bk = pr.next()
                        P.op("pe", [MM(bank[bk][:, 0:256], wsl[sl][:, c, j * 128:(j + 1) * 128], memT[:, c, :], c == 0, c == 15)
                                    for c in range(16)], reads=[wB[sl], memTB], writes=[bankB[bk]])
                        P.op("dve", CP(mkT[:, j, :], bank[bk][:, 0:256]), reads=[bankB[bk]], writes=[mkB])
                else:
                    for mb in range(2):
                        bk = pr.next()
                        P.op("pe", [MM(bank[bk], memT[:, c, mb * 128:(mb + 1) * 128], wsl[sl][:, c, :], c == 0, c == 15)
                                    for c in range(16)], reads=[wB[sl], memTB], writes=[bankB[bk]])
                        P.op("dve", CP(mv[:, mb, :], bank[bk]), reads=[bankB[bk]], writes=[mvB])
            for hm in range(2):
                for dc in range(2):
                    P.dma("sp", sm[f"q{dc}"], qT[dc], proj[MQ + 2 * hm + dc], writes=[qB[dc]])
                    P.dma("sp", sm[f"z{dc}"], mz[dc], proj[MZ + 2 * hm + dc], writes=[zB[dc]])
                for tc in range(8):
                    csl = slice(tc * 512, (tc + 1) * 512)
                    for mb in range(2):
                        bk = pr.next()
                        P.op("pe", [MM(bank[bk], mkT[:, 2 * hm + dc, mb * 128:(mb + 1) * 128], qT[dc][:, csl], dc == 0, dc == 1)
                                    for dc in range(2)], reads=[mkB, qB[0], qB[1]], writes=[bankB[bk]])
                        P.op("act", ACT(Eb[mb], bank[bk], AF.Exp), reads=[bankB[bk]], writes=[EB[mb]])
                    zk = pr.next()
                    P.op("pe", [MM(bank[zk], ones_bf, Eb[mb], mb == 0, mb == 1) for mb in range(2)],
                         reads=[EB[0], EB[1]], writes=[bankB[zk]])
                    P.op("dve", RCP(tm[0], bank[zk]), reads=[bankB[zk]], writes=[tB[0]])
                    for db in range(2):
                        ok = pr.next()
                        c0 = hm * 256 + db * 128
                        P.op("pe", [MM(bank[ok], mv[:, mb, c0:c0 + 128], Eb[mb], mb == 0, mb == 1) for mb in range(2)],
                             reads=[mvB, EB[0], EB[1]], writes=[bankB[ok]])
                        P.op("dve", TT(tm[1], bank[ok], tm[0], ALU.mult), reads=[bankB[ok], tB[0]], writes=[tB[1]])
                        P.op("dve", TT(ostg[db][:, csl], tm[1], mz[db][:, csl], ALU.mult), reads=[tB[1], zB[db]], writes=[oB[db]])
                for db in range(2):
                    P.dma("sp", sm[f"o{db}"], br_loc[(12 + 2 * hm + db) * 128:(13 + 2 * hm + db) * 128, :], ostg[db], reads=[oB[db]])
            P.barrier()
            gather_branch(3)

        def phase_C(l, xown_src, final):
            A.off = base_mark
            brc2 = [A.alloc([4, 8, 512], BF16) for _ in range(2)]
            sg = [A.alloc([4, 512], BF16) for _ in range(3)]
            mT = [A.alloc([8, 512], BF16) for _ in range(2)]
            wbr = A.alloc([4, 8, 1024], BF16)
            tm = [A.alloc([512], F32) for _ in range(4)]
            brB2 = [[Buf() for _ in range(4)] for _ in range(2)]
            sgB = [Buf() for _ in range(3)]
            mTB = [Buf() for _ in range(2)]
            wbB = [[Buf() for _ in range(2)] for _ in range(4)]
            tB = [Buf() for _ in range(4)]
            mglocB = [Buf() for _ in range(4)]
            sm = {k: P.dma_sem() for k in ["b0", "b1", "b2", "b3", "c0", "c1", "c2", "c3", "s0", "s1", "s2", "m0", "m1"]}
            wsm = [[P.dma_sem(sw=True) for _ in range(2)] for _ in range(4)]
            for dg in range(2):
                for n in range(4):
                    P.dma("pool", wsm[n][dg], wbr[:, n, :, dg * 512:(dg + 1) * 512],
                          w_br[l, n].rearrange("(w p) d -> p w d", p=128)[:, :, dg * 512:(dg + 1) * 512], writes=[wbB[n][dg]])
            sg_i = 0
            for tc in range(8):
                csl = slice(tc * 512, (tc + 1) * 512)
                m = mT[tc % 2]
                mB = mTB[tc % 2]
                brc = brc2[tc % 2]
                brB = brB2[tc % 2]
                for n in range(4):
                    P.dma("sp", sm[("b%d" if tc % 2 == 0 else "c%d") % n], brc[:, n, :, :],
                          br_all[n * 2:(n + 1) * 2].rearrange("k (j p) t -> p (k j) t", p=128)[:, :, csl],
                          reads=[brallB[2 * n], brallB[2 * n + 1]], writes=[brB[n]])
                for dg in range(2):
                    for jj in range(4):
                        dl = dg * 4 + jj
                        si = sg_i % 3
                        sg_i += 1
                        P.dma("sp", sm[f"s{si}"], sg[si], proj[GT + dl:GT + 32:8, :, csl].rearrange("n p t -> p n t"), writes=[sgB[si]])
                        bset = (0, 1, 2, 3) if dl % 2 == 0 else (4, 5, 6, 7)
                        for n in range(4):
                            P.op("pe", [MM(bank[bset[n]], wbr[:, n, wc, dl * 128:(dl + 1) * 128], brc[:, n, wc, :], wc == 0, wc == 7)
                                        for wc in range(8)], reads=[wbB[n][dg], brB[n]], writes=[bankB[bset[n]]])
                        for n in range(4):
                            P.op("dve", TT(tm[n], bank[bset[n]], sg[si][:, n, :], ALU.mult), reads=[bankB[bset[n]], sgB[si]], writes=[tB[n]])
                        P.op("pool", TT(tm[0], tm[0], tm[1], ALU.add), reads=[tB[0], tB[1]], writes=[tB[0]])
                        P.op("pool", TT(tm[2], tm[2], tm[3], ALU.add), reads=[tB[2], tB[3]], writes=[tB[2]])
                        P.op("pool", TT(m[:, dl, :], tm[0], tm[2], ALU.add), reads=[tB[0], tB[2]], writes=[mB])
                q = tc // 2
                P.dma("act", sm[f"m{tc % 2}"], mg_loc[q].rearrange("(d p) t -> p d t", p=128)[:, :, (tc % 2) * 512:(tc % 2 + 1) * 512],
                      m, reads=[mB], writes=[mglocB[q]])
                if tc % 2 == 1:
                    P.coll(mgS[q], "AllGather", PAIRS, mg_all[q], mg_loc[q], reads=[mglocB[q]], writes=[mgallB[q]])
            P.barrier()
            A.off = base_mark
            wo_sb = [A.alloc([16, 512], BF16) for _ in range(2)]
            mall = [A.alloc([16, 512], BF16) for _ in range(2)]
            xt = [A.alloc([1024], F32) for _ in range(4)]
            woB = Buf()
            maB = [Buf() for _ in range(2)]
            xB = [Buf() for _ in range(4)]
            xoB = [[Buf() for _ in range(4)] for _ in range(8)]
            sm = {k: P.dma_sem(sw=(k == "w")) for k in ["w", "a0", "a1", "x0", "x1", "x2", "x3"]}
            wo = w_out[l].rearrange("(c p) n -> p c n", p=128)
            for cg in range(2):
                P.dma("pool", sm["w"], wo_sb[cg], wo[:, :, cg * 512:(cg + 1) * 512], writes=[woB])
            yr = Ring([0, 1, 2, 3, 4, 5, 6, 7])
            for tc in range(8):
                q = tc // 2
                ma = mall[tc % 2]
                P.dma("sp", sm[f"a{tc % 2}"], ma, mg_all[q].rearrange("(c p) t -> p c t", p=128)[:, :, (tc % 2) * 512:(tc % 2 + 1) * 512],
                      reads=[mgallB[q]], writes=[maB[tc % 2]])
                for tb in range(4):
                    P.dma("sp", sm[f"x{tb}"], xt[tb], xown_src[tc][tb * 128:(tb + 1) * 128, :], writes=[xB[tb]])
                    for cg in range(2):
                        bk = yr.next()
                        P.op("pe", [MM(bank[bk], ma[:, c, tb * 128:(tb + 1) * 128], wo_sb[cg][:, c, :], c == 0, c == 15) for c in range(16)],
                             reads=[maB[tc % 2], woB], writes=[bankB[bk]])
                        P.op("dve", TT(xt[tb][:, cg * 512:(cg + 1) * 512], xt[tb][:, cg * 512:(cg + 1) * 512], bank[bk], ALU.add),
                             reads=[bankB[bk], xB[tb]], writes=[xB[tb]])
                    P.dma("act", sm[f"x{tb}"], xo[tc][tb * 128:(tb + 1) * 128, :], xt[tb], reads=[xB[tb]], writes=[xoB[tc][tb]])
                P.coll(xfS[tc], "AllGather", PAIRS, xf[tc], xo[tc], reads=xoB[tc], writes=[xfB[tc]])
            P.barrier()
            if final:
                A.off = base_mark
                fg = A.alloc([1024], F32)
                fl = [A.alloc([D], F32) for _ in range(2)]
                ow = [A.alloc([1024], F32) for _ in range(2)]
                junk = A.alloc([D], BF16)
                stat = A.alloc([8], F32)
                fgB, jB = Buf(), Buf()
                flB = [Buf() for _ in range(2)]
                owB = [Buf() for _ in range(2)]
                stB = [Buf() for _ in range(2)]
                sm = {k: P.dma_sem() for k in ["g", "f0", "f1", "o0", "o1"]}
                ssw = [P.dma_sem(sw=True) for _ in range(2)]
                P.dma("sp", sm["g"], fg, fing_bc, writes=[fgB])
                for gtb in range(32):
                    tc, tq = gtb // 4, gtb % 4
                    k = gtb % 2
                    for r in range(2):
                        P.dma("sp", sm[f"f{k}"], fl[k][:, r * 1024:(r + 1) * 1024], xf[tc][r * 512 + tq * 128:r * 512 + (tq + 1) * 128, :],
                              reads=[xfB[tc]], writes=[flB[k]])
                    P.dma("sp", sm[f"o{k}"], ow[k], xo[tc][tq * 128:(tq + 1) * 128, :], writes=[owB[k]])
                    c = k * 3
                    P.op("act", ACT(junk, fl[k], AF.Square, accum_out=stat[:, c:c + 1]), reads=[flB[k]], writes=[jB, stB[k]])
                    P.op("act", ACT(stat[:, c + 1:c + 2], stat[:, c:c + 1], AF.Sqrt, scale=1.0 / D, bias=EPS), reads=[stB[k]], writes=[stB[k]])
                    P.op("dve", RCP(stat[:, c + 2:c + 3], stat[:, c + 1:c + 2]), reads=[stB[k]], writes=[stB[k]])
                    P.op("dve", STT(ow[k], ow[k], stat[:, c + 2:c + 3], fg, ALU.mult, ALU.mult), reads=[owB[k], stB[k], fgB], writes=[owB[k]])
                    P.dma("pool", ssw[k], y_out[tc][tq * 128:(tq + 1) * 128, :], ow[k], reads=[owB[k]])
                P.barrier()

        for l in range(n_layers):
            final = (l == DEPTH - 1)
            if "A" in phases:
                phase_A(l, x_full if l == 0 else xf)
            if "1" in phases:
                phase_DA(l)
            if "2" in phases:
                phase_SB(l)
            if "3" in phases:
                phase_pool(l)
            if "4" in phases:
                phase_mem(l)
            if "C" in phases:
                phase_C(l, x_own if l == 0 else xo, final)
        P.barrier(final=True)
        counts = P.emit(nc, st)
    return nc, counts


def host_constants():
    bf = ml_dtypes.bfloat16
    k = np.arange(128)
    c = {}
    c["c_ident"] = np.eye(128, dtype=np.float32).astype(bf)
    c["c_J"] = np.eye(128, dtype=np.float32)[::-1].copy().astype(bf)
    c["c_trineg"] = (-(k[:, None] >= k[None, :]).astype(np.float32)).astype(bf)
    c["c_ones"] = np.ones((128, 128), np.float32).astype(bf)
    c["c_zeros"] = np.zeros((128, 128), np.float32).astype(bf)
    mt = (k[:, None] < k[None, :]).astype(np.float32)
    c["c_mtri"] = mt.astype(bf)
    c["c_negm"] = (NEG * (1.0 - mt)).astype(np.float32)
    c["c_onesf"] = np.full((128, 128), 1.0 / 128.0, np.float32)
    c["c_ones1f"] = np.ones((128, 128), np.float32)
    n = np.arange(GL) - 511
    oh = np.zeros((33, GL), np.float32)
    bk = t5_bucket_np(n)
    for i in range(GL):
        if n[i] >= 0:
            oh[bk[i], i] = 1.0
        else:
            oh[32, i] = 1.0
    c["c_oh"] = oh
    invc = np.zeros((128, 4, 16), np.float32)
    t = np.arange(16)
    for g, w in enumerate((2, 4, 8, 16)):
        invc[:, g, :] = 1.0 / np.minimum(t + 1, w).astype(np.float32)
    c["c_invc"] = invc
    return c


def _in_cols(r):
    cols = []
    for i in range(8):
        cols.append(i * 1024 + r * 512 + np.arange(512))
    cols.append(8 * 1024 + np.arange(1024))
    for g in range(4):
        cols.append(9 * 1024 + (2 * g + r) * 128 + np.arange(128))
    cols.append(10 * 1024 + r * 512 + np.arange(512))
    cols.append(11 * 1024 + r * 512 + np.arange(512))
    for n in range(4):
        cols.append(12288 + n * 2048 + r * 1024 + np.arange(1024))
    return np.concatenate(cols)


def _branch_rows(r_unused):
    rows = []
    for n in range(4):
        blk = []
        for k in range(2):
            for rr in range(2):
                for i in range(2):
                    loc = 2 * k + i
                    blk.append(2 * loc + rr if n == 2 else 4 * rr + loc)
        rows.append(np.concatenate([b * 128 + np.arange(128) for b in blk]))
    return rows


def host_shared(inputs, r):
    f = lambda a: np.asarray(a, dtype=np.float32)
    m = {}
    m["w_in"] = np.ascontiguousarray(f(inputs["w_in"])[:, :, _in_cols(r)])
    wkv = f(inputs["w_mem_kv"])
    m["w_mem_kv"] = np.ascontiguousarray(np.concatenate([wkv[:, :, r * 512:(r + 1) * 512], wkv[:, :, 1024 + r * 512:1024 + (r + 1) * 512]], axis=2))
    wb = f(inputs["w_branch"])
    rows = _branch_rows(r)
    m["w_branch"] = np.ascontiguousarray(np.stack([wb[:, n][:, rows[n]][:, :, r * 1024:(r + 1) * 1024] for n in range(4)], axis=1))
    m["w_out"] = np.ascontiguousarray(f(inputs["w_out"])[:, :, r * 1024:(r + 1) * 1024])
    m["w_pool"] = np.ascontiguousarray(f(inputs["w_pool"])[:, :, :, r * 128:(r + 1) * 128])
    bc = lambda v: np.ascontiguousarray(np.broadcast_to(f(v)[..., None, :], v.shape[:-1] + (128, v.shape[-1])))
    m["norm_g_bc"] = bc(inputs["norm_g"])
    m["mem_norm_g_bc"] = bc(inputs["mem_norm_g"])
    m["final_g_bc"] = bc(f(inputs["final_g"])[r * 1024:(r + 1) * 1024])
    gbt = f(inputs["gate_b"]).reshape(DEPTH, 4, 2, 8, 128)[:, :, r]
    m["gate_b_t"] = np.ascontiguousarray(gbt.transpose(0, 3, 1, 2).reshape(DEPTH, 128, 32))
    m["da_g_t"] = np.ascontiguousarray(f(inputs["da_norm_g"]).reshape(DEPTH, 8, 128)[:, r * 4:(r + 1) * 4].transpose(0, 2, 1))
    m["pool_scale_t"] = np.ascontiguousarray(f(inputs["pool_scale"]).reshape(DEPTH, 4, 2, 128)[:, :, r].transpose(0, 2, 1))
    lam = np.stack([f(inputs[k]) for k in ("lam_q1", "lam_k1", "lam_q2", "lam_k2")], axis=1)
    m["lam_bc"] = np.ascontiguousarray(np.broadcast_to(lam[:, None], (DEPTH, 128, 4, 64)))
    rb = f(inputs["rel_bias"])[:, r * 4:(r + 1) * 4]
    m["rel_bias"] = np.ascontiguousarray(rb)
    m["rb31_bc"] = np.ascontiguousarray(np.broadcast_to(rb[31][None], (128, 4)))
    return m


def host_acts(inputs, b, r):
    f = lambda a: np.asarray(a, dtype=np.float32)
    x = f(inputs["x"][b])
    m = {}
    m["x_full"] = np.ascontiguousarray(x.reshape(8, 512, 2, 1024).transpose(0, 2, 1, 3).reshape(8, 1024, 1024))
    m["x_own"] = np.ascontiguousarray(x[:, r * 1024:(r + 1) * 1024].reshape(8, 512, 1024))
    m["mem"] = np.ascontiguousarray(f(inputs["mem"][b]))
    return m


_NC_CACHE = {}


def kernel(**inputs):
    if "nc" not in _NC_CACHE:
        _NC_CACHE["nc"] = build_program()[0]
    nc = _NC_CACHE["nc"]
    consts = host_constants()
    shared = [host_shared(inputs, r) for r in range(2)]
    in_maps = []
    for core in range(N_CORES):
        b, r = core // 2, core % 2
        m = dict(shared[r])
        m.update(host_acts(inputs, b, r))
        m.update(consts)
        in_maps.append(m)
    res = run_bass_kernel_spmd(nc, in_maps, core_ids=list(range(N_CORES)))
    out = np.empty((4, S, D), np.float32)
    for core in range(N_CORES):
        b, r = core // 2, core % 2
        out[b][:, r * 1024:(r + 1) * 1024] = np.asarray(res.results[core]["y"], dtype=np.float32).reshape(S, 1024)
    return out
```

```python
import math
from contextlib import ExitStack

import numpy as np
import ml_dtypes

import concourse.bass as bass
import concourse.mybir as mybir
from concourse.bass_utils import run_bass_kernel_spmd

F32 = mybir.dt.float32
BF16 = mybir.dt.bfloat16
U8 = mybir.dt.uint8
AF = mybir.ActivationFunctionType
ALU = mybir.AluOpType
AX = mybir.AxisListType

S = 4096
D = 2048
W = 1024
DEPTH = 2
NCB = 160
EPS = 1e-6
NEG = -30000.0
GL = 1152
ENG = ("pe", "act", "dve", "pool", "sp")
N_CORES = 8
PAIRS = [[0, 1], [2, 3], [4, 5], [6, 7]]
NPB = 84


class Buf:
    __slots__ = ("name", "w", "r")

    def __init__(self, name=""):
        self.name = name
        self.w = {}
        self.r = {}


class DmaSem:
    __slots__ = ("key", "count")

    def __init__(self, key):
        self.key = key
        self.count = 0


class Prog:
    def __init__(self):
        self.ops = {e: [] for e in ENG}
        self.cnt = dict.fromkeys(ENG, 0)
        self.known = {e: {} for e in ENG}
        self.dsems = []
        self.csems = []
        self.swsems = []
        self.sem_i = 0
        self.swsem_i = 0

    def dma_sem(self, sw=False):
        if sw:
            if self.swsem_i == len(self.swsems):
                self.swsems.append(DmaSem("w_%d" % self.swsem_i))
            s = self.swsems[self.swsem_i]
            self.swsem_i += 1
            return s
        if self.sem_i == len(self.dsems):
            self.dsems.append(DmaSem("d_%d" % self.sem_i))
        s = self.dsems[self.sem_i]
        self.sem_i += 1
        return s

    def _waits(self, eng, reads, writes):
        deps = {}
        for b in reads:
            for k, v in b.w.items():
                if deps.get(k, 0) < v:
                    deps[k] = v
        for b in writes:
            for k, v in b.w.items():
                if deps.get(k, 0) < v:
                    deps[k] = v
            for k, v in b.r.items():
                if deps.get(k, 0) < v:
                    deps[k] = v
        kn = self.known[eng]
        waits = []
        for k, v in deps.items():
            if eng == "pe" and k == "pe":
                continue
            if kn.get(k, 0) < v:
                kn[k] = v
                waits.append((k, v))
        return waits

    def op(self, eng, fns, reads=(), writes=()):
        if not isinstance(fns, (list, tuple)):
            fns = [fns]
        waits = self._waits(eng, reads, writes)
        self.cnt[eng] += 1
        v = self.cnt[eng]
        for b in reads:
            b.r[eng] = v
        for b in writes:
            b.w = {eng: v}
            b.r = {}
        n = len(fns)
        for i, fn in enumerate(fns):
            self.ops[eng].append((waits if i == 0 else (), fn, (eng, 1) if i == n - 1 else None))

    def dma(self, q, sem, out, in_, reads=(), writes=()):
        assert (q == "pool") == sem.key.startswith("w_"), (q, sem.key)
        waits = self._waits(q, reads, writes)
        sem.count += 16
        v = sem.count
        for b in reads:
            b.r[sem.key] = v
        for b in writes:
            b.w = {sem.key: v}
            b.r = {}
        self.ops[q].append((waits, lambda e, o=out, i=in_: e.dma_start(out=o, in_=i), (sem.key, 16)))

    def coll_sem(self):
        s = DmaSem("c_%d" % len(self.csems))
        self.csems.append(s)
        return s

    def coll(self, sem, kind, groups, out, in_, reads=(), writes=()):
        waits = self._waits("pool", reads, writes)
        sem.count += 1
        v = sem.count
        for b in reads:
            b.r[sem.key] = v
        for b in writes:
            b.w = {sem.key: v}
            b.r = {}
        self.ops["pool"].append((waits, lambda e, o=out, i=in_: e.collective_compute(
            kind, ALU.bypass, replica_groups=groups, ins=[i], outs=[o]), (sem.key, 1)))

    def barrier(self, final=False):
        snap = {e: self.cnt[e] for e in ENG if self.cnt[e] > 0}
        for s in self.dsems + self.swsems + (self.csems if final else []):
            if s.count > 0:
                snap[s.key] = s.count
        for e in ENG:
            kn = self.known[e]
            waits = []
            for k, v in snap.items():
                if kn.get(k, 0) < v:
                    kn[k] = v
                    waits.append((k, v))
            if waits:
                self.ops[e].append((waits, None, None))
        self.sem_i = 0
        self.swsem_i = 0

    def emit(self, nc, stack):
        handles = {}
        for e in ENG:
            handles[e] = stack.enter_context(nc.semaphore("s_" + e))
        for s in self.dsems + self.swsems + self.csems:
            handles[s.key] = stack.enter_context(nc.semaphore(s.key))
        block = stack.enter_context(nc.Block())
        ops = self.ops

        def run(engine, name):
            for waits, fn, inc in ops[name]:
                for k, v in waits:
                    engine.wait_ge(handles[k], v)
                if fn is not None:
                    ins = fn(engine)
                    if inc is not None:
                        ins.then_inc(handles[inc[0]], inc[1])

        block.tensor(lambda e: run(e, "pe"))
        block.scalar(lambda e: run(e, "act"))
        block.vector(lambda e: run(e, "dve"))
        block.gpsimd(lambda e: run(e, "pool"))
        block.sync(lambda e: run(e, "sp"))
        return {k: len(v) for k, v in ops.items()}


class Ring:
    def __init__(self, items):
        self.items = items
        self.i = 0

    def next(self):
        it = self.items[self.i % len(self.items)]
        self.i += 1
        return it


class Arena:
    def __init__(self, ap, size):
        self.ap = ap
        self.size = size
        self.off = 0

    def alloc(self, free_shape, dt):
        esz = 2 if dt == BF16 else 4
        n = int(np.prod(free_shape)) * esz
        n_al = (n + 63) // 64 * 64
        assert self.off + n_al <= self.size, f"arena overflow {self.off}+{n_al}>{self.size}"
        v = self.ap[:, self.off:self.off + n].bitcast(dt)
        self.off += n_al
        if len(free_shape) == 2:
            v = v.rearrange("p (a b) -> p a b", b=free_shape[1])
        elif len(free_shape) == 3:
            v = v.rearrange("p (a b c) -> p a b c", b=free_shape[1], c=free_shape[2])
        return v


def MM(out, lhsT, rhs, start, stop):
    return lambda e: e.matmul(out, lhsT, rhs, start=start, stop=stop)


def TR(out, in_, ident):
    return lambda e: e.transpose(out, in_, ident)


def ACT(out, in_, func, **kw):
    return lambda e: e.activation(out=out, in_=in_, func=func, **kw)


def TT(out, in0, in1, op):
    return lambda e: e.tensor_tensor(out=out, in0=in0, in1=in1, op=op)


def STT(out, in0, scalar, in1, op0, op1):
    return lambda e: e.scalar_tensor_tensor(out=out, in0=in0, scalar=scalar, in1=in1, op0=op0, op1=op1)


def TS(out, in0, s1, s2, op0, op1):
    return lambda e: e.tensor_scalar(out=out, in0=in0, scalar1=s1, scalar2=s2, op0=op0, op1=op1)


def TSM(out, in0, s1):
    return lambda e: e.tensor_scalar_mul(out=out, in0=in0, scalar1=s1)


def CP(out, in_):
    return lambda e: e.tensor_copy(out=out, in_=in_)


def RCP(out, in_):
    return lambda e: e.reciprocal(out=out, in_=in_)


def MSET(out, val):
    return lambda e: e.memset(out, val)


def t5_bucket_np(n):
    n = np.maximum(n, 0)
    nf = np.maximum(n, 1).astype(np.float32)
    large = 16 + (np.log(nf / np.float32(16)) / np.float32(math.log(128 / 16)) * np.float32(16)).astype(np.int32)
    large = np.minimum(large, 31)
    return np.where(n < 16, n, large)


DAQ, DAK, DAV, DAZ, SBQ, SBK, SBV, SBZ, PU, PZ, MQ, MZ, GT = 0, 4, 8, 12, 16, 20, 24, 28, 32, 40, 44, 48, 52
SILU_BLOCKS = set(range(12, 16)) | set(range(28, 32)) | set(range(40, 44)) | set(range(48, 52))


def build_program(n_layers=DEPTH, phases="A1234C", debug=False):
    nc = bass.Bass("TRN2", target_bir_lowering=False)
    P = Prog()

    def din(name, shape, dt=F32):
        return nc.dram_tensor(name, shape, dt, kind="ExternalInput").ap()

    x_full = din("x_full", [8, 1024, 1024])
    x_own = din("x_own", [8, 512, 1024])
    mem_in = din("mem", [256, D])
    w_in = din("w_in", [DEPTH, D, NPB * 128])
    w_mkv = din("w_mem_kv", [DEPTH, D, 1024])
    w_br = din("w_branch", [DEPTH, 4, W, 1024])
    w_out = din("w_out", [DEPTH, D, 1024])
    w_pool = din("w_pool", [DEPTH, 4, 256, 128])
    norm_g_bc = din("norm_g_bc", [DEPTH, 128, D])
    memg_bc = din("mem_norm_g_bc", [DEPTH, 128, D])
    fing_bc = din("final_g_bc", [128, 1024])
    gate_b_t = din("gate_b_t", [DEPTH, 128, 32])
    da_g_t = din("da_g_t", [DEPTH, 128, 4])
    psc_t = din("pool_scale_t", [DEPTH, 128, 4])
    lam_bc = din("lam_bc", [DEPTH, 128, 4, 64])
    rel_bias = din("rel_bias", [32, 4])
    rb31_bc = din("rb31_bc", [128, 4])
    c_ident = din("c_ident", [128, 128], BF16)
    c_J = din("c_J", [128, 128], BF16)
    c_trineg = din("c_trineg", [128, 128], BF16)
    c_ones = din("c_ones", [128, 128], BF16)
    c_zeros = din("c_zeros", [128, 128], BF16)
    c_mtri = din("c_mtri", [128, 128], BF16)
    c_negm = din("c_negm", [128, 128], F32)
    c_onesf = din("c_onesf", [128, 128], F32)
    c_ones1f = din("c_ones1f", [128, 128], F32)
    c_oh = din("c_oh", [33, GL], F32)
    c_invc = din("c_invc", [128, 4, 16], F32)

    y_out = nc.dram_tensor("y", [8, 512, 1024], F32, kind="ExternalOutput").ap()
    skind = dict(kind="ExternalOutput") if debug else {}
    proj = nc.dram_tensor("proj", [NPB, 128, S], BF16, **skind).ap()
    br_loc = nc.dram_tensor("br_loc", [16 * 128, S], BF16, **skind).ap()
    br_all = nc.dram_tensor("br_all", [8, 512, S], BF16).ap()
    mg_loc = nc.dram_tensor("mg_loc", [4, 1024, 1024], BF16).ap()
    mg_all = nc.dram_tensor("mg_all", [4, 2048, 1024], BF16).ap()
    xo = nc.dram_tensor("xo", [8, 512, 1024], F32, **skind).ap()
    xf = nc.dram_tensor("xf", [8, 1024, 1024], F32).ap()
    gs_t = nc.dram_tensor("gs", [4, GL], BF16)

    with ExitStack() as st:
        ARENA_BYTES = 200 * 1024
        arena_t = st.enter_context(nc.sbuf_tensor("arena", [128, ARENA_BYTES], U8))
        ps = st.enter_context(nc.psum_tensor("ps", [128, 8, 512], F32))
        A = Arena(arena_t, ARENA_BYTES)
        bank = [ps[:, i, :] for i in range(8)]
        bankB = [Buf(f"bank{i}") for i in range(8)]
        psb = ps[:, 7, :].bitcast(BF16)
        psb6 = ps[:, 6, :].bitcast(BF16)

        brallB = [Buf(f"brall{i}") for i in range(8)]
        mgallB = [Buf(f"mgall{i}") for i in range(4)]
        xfB = [Buf(f"xf{i}") for i in range(8)]
        brS = [P.coll_sem() for _ in range(8)]
        mgS = [P.coll_sem() for _ in range(4)]
        xfS = [P.coll_sem() for _ in range(8)]

        ident = A.alloc([128], BF16)
        Jm = A.alloc([128], BF16)
        trineg = A.alloc([128], BF16)
        ones_bf = A.alloc([128], BF16)
        zeros_bf = A.alloc([128], BF16)
        mtri = A.alloc([128], BF16)
        negm = A.alloc([128], F32)
        ones_f = A.alloc([128], F32)
        ones1_f = A.alloc([128], F32)
        rb31 = A.alloc([4], F32)
        invc = A.alloc([4, 16], F32)
        cs = P.dma_sem()
        for dst, src in ((ident, c_ident), (Jm, c_J), (trineg, c_trineg), (ones_bf, c_ones), (zeros_bf, c_zeros),
                         (mtri, c_mtri), (negm, c_negm), (ones_f, c_onesf), (ones1_f, c_ones1f), (rb31, rb31_bc), (invc, c_invc)):
            P.dma("sp", cs, dst, src)
        P.barrier()
        base_mark = A.off

        if "1" in phases:
            lhs = A.alloc([4], F32)
            oh = A.alloc([GL], F32)
            gsb = A.alloc([GL], BF16)
            tB = Buf("t5")
            ts_ = P.dma_sem()
            P.dma("sp", ts_, lhs[0:32, :], rel_bias, writes=[tB])
            P.dma("sp", ts_, oh[0:33, :], c_oh, writes=[tB])
            P.op("dve", TSM(lhs[0:32, :], lhs[0:32, :], 8.0), reads=[tB], writes=[tB])
            P.op("dve", MSET(lhs[32:33, :], NEG), writes=[tB])
            for i, (c0, c1) in enumerate(((0, 512), (512, 1024), (1024, GL))):
                P.op("pe", MM(bank[i][0:4, 0:c1 - c0], lhs[0:33, :], oh[0:33, c0:c1], True, True), reads=[tB], writes=[bankB[i]])
                P.op("dve", CP(gsb[0:4, c0:c1], bank[i][0:4, 0:c1 - c0]), reads=[bankB[i]], writes=[tB])
            P.dma("sp", ts_, gs_t.ap(), gsb[0:4, :], reads=[tB])
            P.barrier()
            A.off = base_mark

        def phase_A(l, xsrc):
            A.off = base_mark
            hT = A.alloc([16, 2048], BF16)
            wsl = [A.alloc([16, 512], BF16) for _ in range(3)]
            stg = [A.alloc([2048], F32) for _ in range(3)]
            hbf = [A.alloc([2048], BF16) for _ in range(2)]
            gbc = A.alloc([2048], F32)
            gb = A.alloc([32], F32)
            stat = A.alloc([64], F32)
            hTB = [Buf(f"hT{i}") for i in range(4)]
            wslB = [Buf() for _ in range(3)]
            wslS = [P.dma_sem(sw=True) for i in range(3)]
            stgB = [[Buf() for _ in range(4)] for _ in range(3)]
            stgS = [P.dma_sem() for i in range(3)]
            hbfB = [Buf() for _ in range(2)]
            statB = [Buf() for _ in range(16)]
            gB = Buf()
            ms = P.dma_sem()
            P.dma("sp", ms, gbc, norm_g_bc[l], writes=[gB])
            P.dma("sp", ms, gb, gate_b_t[l], writes=[gB])
            w_l = w_in[l].rearrange("(c p) n -> p c n", p=128)
            pring = Ring([0, 1, 2, 3, 4, 5])
            stg_i = 0
            for hf in range(2):
                for tb in range(16):
                    gtb = hf * 16 + tb
                    tc, tq = gtb // 4, gtb % 4
                    s = stg_i % 3
                    stg_i += 1
                    hs = tb % 2
                    sB = stgB[s]
                    for r in range(2):
                        P.dma("sp", stgS[s], stg[s][:, r * 1024:(r + 1) * 1024],
                              xsrc[tc][r * 512 + tq * 128:r * 512 + (tq + 1) * 128, :],
                              reads=[xfB[tc]] if l > 0 else [], writes=sB)
                    c = tb * 3
                    P.op("act", ACT(hbf[hs], stg[s], AF.Square, accum_out=stat[:, c:c + 1]),
                         reads=sB, writes=[hbfB[hs], statB[tb]])
                    P.op("act", ACT(stat[:, c + 1:c + 2], stat[:, c:c + 1], AF.Sqrt, scale=1.0 / D, bias=EPS),
                         reads=[statB[tb]], writes=[statB[tb]])
                    P.op("dve", RCP(stat[:, c + 2:c + 3], stat[:, c + 1:c + 2]), reads=[statB[tb]], writes=[statB[tb]])
                    P.op("dve", STT(hbf[hs], stg[s], stat[:, c + 2:c + 3], gbc, ALU.mult, ALU.mult),
                         reads=sB + [statB[tb], gB], writes=[hbfB[hs]])
                    for g in range(2):
                        pb, pB = (psb6, bankB[6]) if g == 0 else (psb, bankB[7])
                        P.op("pe", [TR(pb[:, i * 128:(i + 1) * 128], hbf[hs][:, (g * 8 + i) * 128:(g * 8 + i + 1) * 128], ident)
                                    for i in range(8)], reads=[hbfB[hs]], writes=[pB])
                        src = pb.rearrange("p (a b) -> p a b", b=128)
                        dst = hT[:, g * 8:(g + 1) * 8, tb * 128:(tb + 1) * 128]
                        if g == 0:
                            P.op("act", ACT(dst, src, AF.Copy), reads=[pB], writes=[hTB[tb // 4]])
                        else:
                            P.op("dve", CP(dst, src), reads=[pB], writes=[hTB[tb // 4]])
                for ws in range(NPB // 4):
                    sl = ws % 3
                    P.dma("pool", wslS[sl], wsl[sl], w_l[:, :, ws * 512:(ws + 1) * 512], writes=[wslB[sl]])
                    for j in range(4):
                        cb = ws * 4 + j
                        s = stg_i % 3
                        stg_i += 1
                        ostg = stg[s].bitcast(BF16)
                        for tcl in range(4):
                            bk = pring.next()
                            P.op("pe", [MM(bank[bk], wsl[sl][:, c, j * 128:(j + 1) * 128], hT[:, c, tcl * 512:(tcl + 1) * 512],
                                           c == 0, c == 15) for c in range(16)],
                                 reads=[wslB[sl], hTB[tcl]], writes=[bankB[bk]])
                            dst = ostg[:, tcl * 512:(tcl + 1) * 512]
                            oB = [stgB[s][tcl]]
                            if cb >= GT:
                                P.op("act", ACT(dst, bank[bk], AF.Sigmoid, bias=gb[:, cb - GT:cb - GT + 1]),
                                     reads=[bankB[bk], gB], writes=oB)
                            elif cb in SILU_BLOCKS:
                                P.op("act", ACT(dst, bank[bk], AF.Silu), reads=[bankB[bk]], writes=oB)
                            elif SBQ <= cb < SBQ + 4:
                                P.op("dve", TSM(dst, bank[bk], 128.0 ** -0.5), reads=[bankB[bk]], writes=oB)
                            elif MQ <= cb < MQ + 4:
                                P.op("dve", TSM(dst, bank[bk], 1.0 / 16.0), reads=[bankB[bk]], writes=oB)
                            else:
                                P.op("dve", CP(dst, bank[bk]), reads=[bankB[bk]], writes=oB)
                        P.dma("sp", stgS[s], proj[cb][:, hf * 2048:(hf + 1) * 2048], ostg[:, 0:2048], reads=stgB[s])
            P.barrier()

        def gather_branch(n):
            for k in range(2):
                i = n * 2 + k
                r0 = (n * 4 + 2 * k) * 128
                P.coll(brS[i], "AllGather", PAIRS, br_all[i], br_loc[r0:r0 + 256, :], writes=[brallB[i]])

        def attn_common_alloc():
            A.off = base_mark
            d = {}
            for nme in ("qT", "kT", "vT", "zT", "ostg"):
                d[nme] = [A.alloc([S], BF16) for _ in range(2)]
            d["vtok"] = [A.alloc([32, 128], BF16) for _ in range(2)]
            return d

        def load_head(d, hs, cbs, hB, hS, transposes=True, ring=None):
            for name, cb in zip(("qT", "kT", "vT", "zT"), cbs):
                P.dma("sp", hS[hs][name], d[name][hs], proj[cb], writes=[hB[hs][name]])
            if transposes:
                head_transposes(d, hs, hB, ring)

        def head_transposes(d, hs, hB, ring=None):
            for g in range(4):
                bk = 7 if ring is None else ring.next()
                pb = ps[:, bk, :].bitcast(BF16)
                P.op("pe", [TR(pb[:, i * 128:(i + 1) * 128], d["vT"][hs][:, (g * 8 + i) * 128:(g * 8 + i + 1) * 128], ident)
                            for i in range(8)], reads=[hB[hs]["vT"]], writes=[bankB[bk]])
                P.op("dve", CP(d["vtok"][hs][:, g * 8:(g + 1) * 8, :], pb.rearrange("p (a b) -> p a b", b=128)),
                     reads=[bankB[bk]], writes=[hB[hs]["vtok"]])

        NH = 4

        def phase_DA(l):
            d = attn_common_alloc()
            lam_init = 0.8 - 0.6 * math.exp(-0.3 * l)
            Tp = A.alloc([NH, 1024], BF16)
            Er = [A.alloc([512], BF16) for _ in range(12)]
            tmp = [[A.alloc([512], F32) for _ in range(3)] for _ in range(2)]
            lamt = A.alloc([4, 64], F32)
            lsm = A.alloc([8], F32)
            gsc = A.alloc([NH], F32)
            names = ("qT", "kT", "vT", "zT", "vtok", "ostg")
            hB = [{n: Buf(n) for n in names} for _ in range(2)]
            hS = [{n: P.dma_sem() for n in names} for i in range(2)]
            ErB = [Buf() for _ in range(12)]
            tmpB = [[Buf() for _ in range(3)] for _ in range(2)]
            sB = Buf()
            ss = P.dma_sem()
            P.dma("sp", ss, lamt, lam_bc[l], writes=[sB])
            P.dma("sp", ss, gsc, da_g_t[l], writes=[sB])
            for h in range(NH):
                P.dma("sp", ss, Tp[:, h, :], bass.AP(gs_t, h * GL, [[1, 128], [1, 1024]]), writes=[sB])
            P.op("dve", TT(lamt[:, 0, :], lamt[:, 0, :], lamt[:, 1, :], ALU.mult), reads=[sB], writes=[sB])
            P.op("dve", TT(lamt[:, 2, :], lamt[:, 2, :], lamt[:, 3, :], ALU.mult), reads=[sB], writes=[sB])
            P.op("dve", lambda e: e.reduce_sum(out=lsm[:, 0:1], in_=lamt[:, 0, :], axis=AX.X), reads=[sB], writes=[sB])
            P.op("dve", lambda e: e.reduce_sum(out=lsm[:, 1:2], in_=lamt[:, 2, :], axis=AX.X), reads=[sB], writes=[sB])
            P.op("act", ACT(lsm[:, 2:4], lsm[:, 0:2], AF.Exp), reads=[sB], writes=[sB])
            P.op("dve", TT(lsm[:, 4:5], lsm[:, 3:4], lsm[:, 2:3], ALU.subtract), reads=[sB], writes=[sB])
            P.op("dve", lambda e: e.tensor_scalar_add(out=lsm[:, 5:6], in0=lsm[:, 4:5], scalar1=-lam_init), reads=[sB], writes=[sB])
            P.op("dve", TSM(gsc, gsc, 1.0 - lam_init), reads=[sB], writes=[sB])
            neglam = lsm[:, 5:6]

            Zc = [[A.alloc([512], F32) for _ in range(3)] for _ in range(2)]
            ZcB = [[Buf() for _ in range(3)] for _ in range(2)]
            acc_sets = ((0, 1), (2, 3))
            sring = Ring([4, 5, 6])
            ering = Ring(list(range(12)))
            load_head(d, 0, (DAQ, DAK, DAV, DAZ), hB, hS, ring=sring)
            for h in range(NH):
                hs = h % 2
                if h + 1 < NH:
                    load_head(d, (h + 1) % 2, (DAQ + h + 1, DAK + h + 1, DAV + h + 1, DAZ + h + 1), hB, hS, transposes=False)
                qT, kT, vtok, zT, ostg = d["qT"][hs], d["kT"][hs], d["vtok"][hs], d["zT"][hs], d["ostg"][hs]
                B = hB[hs]
                tiles = [(qc, kb) for qc in range(8) for kb in range(4 * qc + 4)]
                state = {}
                pending = []

                def stage0(qc, kb):
                    j = kb - 4 * qc
                    qs = max(j, 0) * 128
                    near = j >= -1
                    es = []
                    for c in range(2):
                        bk = sring.next()
                        fns = [MM(bank[bk][:, qs:512], kT[c * 64:(c + 1) * 64, kb * 128:(kb + 1) * 128],
                                  qT[c * 64:(c + 1) * 64, qc * 512 + qs:(qc + 1) * 512], True, not near)]
                        if near:
                            off = 128 * (3 - j)
                            fns.append(MM(bank[bk][:, qs:512], Jm, Tp[:, h, off + qs:off + 512], False, True))
                        P.op("pe", fns, reads=[B["kT"], B["qT"], sB], writes=[bankB[bk]])
                        ei = ering.next()
                        bias = 0.0 if near else rb31[:, h:h + 1]
                        P.op("act", ACT(Er[ei][:, qs:512], bank[bk][:, qs:512], AF.Exp, scale=0.125, bias=bias),
                             reads=[bankB[bk]], writes=[ErB[ei]])
                        es.append(ei)
                    state[(qc, kb)] = (qs, es)

                def stage1(qc, kb):
                    qs, es = state.pop((qc, kb))
                    first = kb == 0
                    last = kb == 4 * qc + 3
                    k = qc % 2
                    acc_o = acc_sets[k]
                    if first:
                        P.op("dve", MSET(Zc[k][1], 0.0), writes=[ZcB[k][1]])
                    for c in range(2):
                        E = Er[es[c]]
                        if c == 0:
                            E1 = Er[es[1]]
                            P.op("pe", [MM(bank[acc_o[0]][:, qs:512], vtok[:, kb, :], E[:, qs:512], first, last),
                                        MM(bank[acc_o[1]][:, qs:512], vtok[:, kb, :], E1[:, qs:512], first, last),
                                        MM(bank[7][:, qs:512], ones_bf, E[:, qs:512], first, last)],
                                 reads=[B["vtok"], ErB[es[0]], ErB[es[1]]], writes=[bankB[acc_o[0]], bankB[acc_o[1]], bankB[7]])
                            continue
                        if kb % 2 == 0:
                            eng, zi = "pool", 2
                        else:
                            eng, zi = "dve", 1
                        if first:
                            P.op(eng, CP(Zc[k][zi], E), reads=[ErB[es[c]]], writes=[ZcB[k][zi]])
                        else:
                            P.op(eng, TT(Zc[k][zi][:, qs:512], Zc[k][zi][:, qs:512], E[:, qs:512], ALU.add),
                                 reads=[ErB[es[c]], ZcB[k][zi]], writes=[ZcB[k][zi]])
                    if last:
                        t0_, b0_ = tmp[k][0], tmpB[k][0]
                        P.op("dve", RCP(t0_, bank[7]), reads=[bankB[7]], writes=[b0_])
                        P.op("dve", TT(t0_, bank[acc_o[0]], t0_, ALU.mult), reads=[bankB[acc_o[0]], b0_], writes=[b0_])
                        pending.append([2, lambda qc=qc: combine(qc)])

                def combine(qc):
                    k = qc % 2
                    acc_o = acc_sets[k]
                    t0, t1, t2 = tmp[k]
                    b0, b1, b2 = tmpB[k]
                    z1 = sring.next()
                    P.op("pe", [MM(bank[z1], ones1_f, Zc[k][1], True, False), MM(bank[z1], ones1_f, Zc[k][2], False, True)],
                         reads=[ZcB[k][1], ZcB[k][2]], writes=[bankB[z1]])
                    P.op("dve", RCP(t1, bank[z1]), reads=[bankB[z1]], writes=[b1])
                    P.op("dve", TT(t1, bank[acc_o[1]], t1, ALU.mult), reads=[bankB[acc_o[1]], b1], writes=[b1])
                    P.op("dve", STT(t0, t1, neglam, t0, ALU.mult, ALU.add), reads=[b0, b1, sB], writes=[b0])
                    P.op("dve", TT(t2, t0, t0, ALU.mult), reads=[b0], writes=[b2])

                    def part2():
                        bk = sring.next()
                        P.op("pe", MM(bank[bk], ones_f, t2, True, True), reads=[b2], writes=[bankB[bk]])
                        P.op("act", ACT(t2, bank[bk], AF.Ln, bias=EPS), reads=[bankB[bk]], writes=[b2])
                        P.op("act", ACT(t2, t2, AF.Exp, scale=-0.5), reads=[b2], writes=[b2])
                        P.op("dve", TT(t0, t0, t2, ALU.mult), reads=[b0, b2], writes=[b0])
                        P.op("dve", STT(ostg[:, qc * 512:(qc + 1) * 512], t0, gsc[:, h:h + 1], zT[:, qc * 512:(qc + 1) * 512],
                                        ALU.mult, ALU.mult), reads=[b0, sB, B["zT"]], writes=[B["ostg"]])
                    pending.append([3, part2])

                n = len(tiles)
                for s_ in range(n + 1):
                    if s_ < n:
                        stage0(*tiles[s_])
                    if s_ >= 1:
                        stage1(*tiles[s_ - 1])
                    if s_ == n // 2 and h + 1 < NH:
                        head_transposes(d, (h + 1) % 2, hB, sring)
                    for it in pending:
                        it[0] -= 1
                    for it in [it for it in pending if it[0] <= 0]:
                        pending.remove(it)
                        it[1]()
                while pending:
                    it = pending.pop(0)
                    it[1]()
                P.dma("sp", hS[hs]["ostg"], br_loc[(0 + h) * 128:(1 + h) * 128, :], ostg, reads=[B["ostg"]])
            P.barrier()
            gather_branch(0)

        def phase_SB(l):
            d = attn_common_alloc()
            ebuf = [A.alloc([512], F32) for _ in range(2)]
            spb = [A.alloc([512], BF16) for _ in range(4)]
            argb = [A.alloc([512], F32) for _ in range(3)]
            Ab = [A.alloc([512], BF16) for _ in range(5)]
            Rsb = [A.alloc([512], F32) for _ in range(2)]
            names = ("qT", "kT", "vT", "zT", "vtok", "ostg")
            hB = [{n: Buf(n) for n in names} for _ in range(2)]
            hS = [{n: P.dma_sem() for n in names} for i in range(2)]
            ebufB = [Buf() for _ in range(2)]
            spB = [Buf() for _ in range(4)]
            argB = [Buf() for _ in range(3)]
            AbB = [Buf() for _ in range(5)]
            RsB = [Buf() for _ in range(2)]
            zEring = Ring([0, 1])
            zring = Ring([2, 3])
            cring = Ring([4, 5])
            acc = (6, 7)
            e_r, sp_r, arg_r, A_r = Ring([0, 1]), Ring([0, 1, 2, 3]), Ring([0, 1, 2]), Ring([0, 1, 2, 3, 4])
            load_head(d, 0, (SBQ, SBK, SBV, SBZ), hB, hS, ring=cring)
            chunk_ctr = [0]
            for h in range(NH):
                hs = h % 2
                if h + 1 < NH:
                    load_head(d, (h + 1) % 2, (SBQ + h + 1, SBK + h + 1, SBV + h + 1, SBZ + h + 1), hB, hS, transposes=False)
                qT, kT, vtok, zT, ostg = d["qT"][hs], d["kT"][hs], d["vtok"][hs], d["zT"][hs], d["ostg"][hs]
                B = hB[hs]
                tiles = [(qc, kb) for qc in range(8) for kb in reversed(range(4 * qc + 4))]
                state = {}

                def stage0(qc, kb):
                    j = kb - 4 * qc
                    qs = max(j, 0) * 128
                    bkE = zEring.next()
                    P.op("pe", MM(bank[bkE][:, qs:512], kT[:, kb * 128:(kb + 1) * 128], qT[:, qc * 512 + qs:(qc + 1) * 512],
                                  True, True), reads=[B["kT"], B["qT"]], writes=[bankB[bkE]])
                    ei, si = e_r.next(), sp_r.next()
                    P.op("act", ACT(ebuf[ei][:, qs:512], bank[bkE][:, qs:512], AF.Exp), reads=[bankB[bkE]], writes=[ebufB[ei]])
                    P.op("act", ACT(spb[si][:, qs:512], ebuf[ei][:, qs:512], AF.Ln, bias=1.0), reads=[ebufB[ei]], writes=[spB[si]])
                    if j >= 0:
                        P.op("dve", TT(spb[si][:, qs:qs + 128], spb[si][:, qs:qs + 128], mtri, ALU.mult),
                             reads=[spB[si]], writes=[spB[si]])
                    state[(qc, kb)] = dict(qs=qs, si=si, j=j)

                def stage1(qc, kb):
                    stt_ = state[(qc, kb)]
                    qs, si, j = stt_["qs"], stt_["si"], stt_["j"]
                    first = kb == 4 * qc + 3
                    if first:
                        chunk_ctr[0] += 1
                    R = Rsb[chunk_ctr[0] % 2]
                    RB = RsB[chunk_ctr[0] % 2]
                    if first:
                        P.op("dve", MSET(R, 0.0), writes=[RB])
                    bk = zring.next()
                    ck = cring.next()
                    P.op("pe", [MM(bank[bk][:, qs:512], kT[:, kb * 128:(kb + 1) * 128], qT[:, qc * 512 + qs:(qc + 1) * 512], True, False),
                                MM(bank[bk][:, qs:512], trineg, spb[si][:, qs:512], False, True),
                                MM(bank[ck][:, qs:512], ones_bf, spb[si][:, qs:512], True, True)],
                         reads=[B["kT"], B["qT"], spB[si]], writes=[bankB[bk], bankB[ck]])
                    ai = arg_r.next()
                    P.op("dve", TT(argb[ai][:, qs:512], bank[bk][:, qs:512], R[:, qs:512], ALU.subtract),
                         reads=[bankB[bk], RB], writes=[argB[ai]])
                    if j >= 0:
                        P.op("dve", TT(argb[ai][:, qs:qs + 128], argb[ai][:, qs:qs + 128], negm, ALU.add),
                             reads=[argB[ai]], writes=[argB[ai]])
                    if kb > 0:
                        P.op("dve", TT(R[:, qs:512], R[:, qs:512], bank[ck][:, qs:512], ALU.add), reads=[bankB[ck], RB], writes=[RB])
                    Ai = A_r.next()
                    P.op("act", ACT(Ab[Ai][:, qs:512], argb[ai][:, qs:512], AF.Exp), reads=[argB[ai]], writes=[AbB[Ai]])
                    stt_["Ai"] = Ai

                def stage2(qc, kb):
                    stt_ = state.pop((qc, kb))
                    qs, Ai = stt_["qs"], stt_["Ai"]
                    a = acc[qc % 2]
                    first = kb == 4 * qc + 3
                    last = kb == 0
                    fns = []
                    if first:
                        fns.append(MM(bank[a], zeros_bf, qT[:, 0:512], True, False))
                    fns.append(MM(bank[a][:, qs:512], vtok[:, kb, :], Ab[Ai][:, qs:512], False, last))
                    P.op("pe", fns, reads=[B["vtok"], AbB[Ai], B["qT"]], writes=[bankB[a]])
                    if last:
                        P.op("dve", TT(ostg[:, qc * 512:(qc + 1) * 512], bank[a], zT[:, qc * 512:(qc + 1) * 512], ALU.mult),
                             reads=[bankB[a], B["zT"]], writes=[B["ostg"]])

                n = len(tiles)
                L1, L2 = 2, 4
                for s_ in range(n + L2):
                    if s_ < n:
                        stage0(*tiles[s_])
                    if L1 <= s_ < n + L1:
                        stage1(*tiles[s_ - L1])
                    if s_ >= L2:
                        stage2(*tiles[s_ - L2])
                    if s_ == n // 2 and h + 1 < NH:
                        head_transposes(d, (h + 1) % 2, hB, cring)
                P.dma("sp", hS[hs]["ostg"], br_loc[(4 + h) * 128:(5 + h) * 128, :], ostg, reads=[B["ostg"]])
            P.barrier()
            gather_branch(1)

        def phase_pool(l):
            A.off = base_mark
            PADL = 16
            ubf = [A.alloc([S], BF16) for _ in range(2)]
            pz = A.alloc([S], BF16)
            pooled = [A.alloc([S], BF16) for _ in range(2)]
            X = [A.alloc([PADL + S], F32) for _ in range(3)]
            ostg = A.alloc([S], BF16)
            wp = A.alloc([2, 128], BF16)
            psc = A.alloc([4], F32)
            t16 = A.alloc([16], F32)
            ubB = [Buf() for _ in range(2)]
            pzB = Buf()
            poB = [Buf() for _ in range(2)]
            XB = [Buf() for _ in range(3)]
            osB = Buf()
            wpB, pscB, t16B = Buf(), Buf(), Buf()
            sems = {k: P.dma_sem(sw=(k == "w")) for k in ("u0", "u1", "z", "o", "w", "m")}
            P.dma("sp", sems["m"], psc, psc_t[l], writes=[pscB])
            for i in range(3):
                P.op("dve", MSET(X[i][:, 0:PADL], 0.0), writes=[XB[i]])
            pr = Ring([0, 1, 2, 3, 4, 5])
            for g in range(4):
                wwin = 2 ** (g + 1)
                P.dma("pool", sems["w"], wp, w_pool[l, g].rearrange("(c p) n -> p c n", p=128), writes=[wpB])
                for cb in range(2):
                    P.dma("sp", sems[f"u{cb}"], ubf[cb], proj[PU + 2 * g + cb], writes=[ubB[cb]])
                P.dma("sp", sems["z"], pz, proj[PZ + g], writes=[pzB])
                for cb in range(2):
                    for hh in range(2):
                        sl_ = slice(PADL + hh * 2048, PADL + (hh + 1) * 2048)
                        P.op("dve", CP(X[0][:, sl_], ubf[cb][:, hh * 2048:(hh + 1) * 2048]), reads=[ubB[cb]], writes=[XB[0]])
                    cur = 0
                    for lev in range(g + 1):
                        sh = 2 ** lev
                        nxt = 1 if cur != 1 else 2
                        for hh in range(2):
                            a0 = PADL + hh * 2048
                            P.op("dve", TT(X[nxt][:, a0:a0 + 2048], X[cur][:, a0:a0 + 2048], X[cur][:, a0 - sh:a0 + 2048 - sh], ALU.add),
                                 reads=[XB[cur]], writes=[XB[nxt]])
                        cur = nxt
                    for hh in range(2):
                        a0 = PADL + hh * 2048
                        P.op("dve", STT(pooled[cb][:, hh * 2048:(hh + 1) * 2048], X[cur][:, a0:a0 + 2048], 1.0 / wwin,
                                        X[0][:, a0:a0 + 2048], ALU.mult, ALU.subtract), reads=[XB[cur], XB[0]], writes=[poB[cb]])
                    P.op("dve", TT(t16, X[cur][:, PADL:PADL + 16], invc[:, g, :], ALU.mult), reads=[XB[cur]], writes=[t16B])
                    P.op("dve", TT(pooled[cb][:, 0:16], t16, X[0][:, PADL:PADL + 16], ALU.subtract), reads=[t16B, XB[0]], writes=[poB[cb]])
                for tc in range(8):
                    bk = pr.next()
                    P.op("pe", [MM(bank[bk], wp[:, c, :], pooled[c][:, tc * 512:(tc + 1) * 512], c == 0, c == 1)
                                for c in range(2)], reads=[wpB, poB[0], poB[1]], writes=[bankB[bk]])
                    P.op("dve", STT(ostg[:, tc * 512:(tc + 1) * 512], bank[bk], psc[:, g:g + 1],
                                    pz[:, tc * 512:(tc + 1) * 512], ALU.mult, ALU.mult),
                         reads=[bankB[bk], pscB, pzB], writes=[osB])
                P.dma("sp", sems["o"], br_loc[(8 + g) * 128:(9 + g) * 128, :], ostg, reads=[osB])
            P.barrier()
            gather_branch(2)

        def phase_mem(l):
            A.off = base_mark
            mstg = [A.alloc([D], F32) for _ in range(2)]
            mh = [A.alloc([D], BF16) for _ in range(2)]
            gbc = A.alloc([D], F32)
            stat = A.alloc([8], F32)
            memT = A.alloc([16, 256], BF16)
            wsl = [A.alloc([16, 512], BF16) for _ in range(2)]
            mkT = A.alloc([4, 256], BF16)
            mv = A.alloc([2, 512], BF16)
            qT = [A.alloc([S], BF16) for _ in range(2)]
            mz = [A.alloc([S], BF16) for _ in range(2)]
            ostg = [A.alloc([S], BF16) for _ in range(2)]
            Eb = [A.alloc([512], BF16) for _ in range(2)]
            tm = [A.alloc([512], F32) for _ in range(2)]
            mB = [Buf() for _ in range(2)]
            mhB = [Buf() for _ in range(2)]
            gB, stB, memTB, mkB, mvB = Buf(), Buf(), Buf(), Buf(), Buf()
            wB = [Buf() for _ in range(2)]
            qB = [Buf() for _ in range(2)]
            zB = [Buf() for _ in range(2)]
            oB = [Buf() for _ in range(2)]
            EB = [Buf() for _ in range(2)]
            tB = [Buf() for _ in range(2)]
            sm = {k: P.dma_sem(sw=k.startswith("w")) for k in ("m0", "m1", "g", "w0", "w1", "q0", "q1", "z0", "z1", "o0", "o1")}
            P.dma("sp", sm["g"], gbc, memg_bc[l], writes=[gB])
            for mb in range(2):
                P.dma("sp", sm[f"m{mb}"], mstg[mb], mem_in[mb * 128:(mb + 1) * 128, :], writes=[mB[mb]])
                c = mb * 3
                P.op("act", ACT(mh[mb], mstg[mb], AF.Square, accum_out=stat[:, c:c + 1]), reads=[mB[mb]], writes=[mhB[mb], stB])
                P.op("act", ACT(stat[:, c + 1:c + 2], stat[:, c:c + 1], AF.Sqrt, scale=1.0 / D, bias=EPS), reads=[stB], writes=[stB])
                P.op("dve", RCP(stat[:, c + 2:c + 3], stat[:, c + 1:c + 2]), reads=[stB], writes=[stB])
                P.op("dve", STT(mh[mb], mstg[mb], stat[:, c + 2:c + 3], gbc, ALU.mult, ALU.mult),
                     reads=[mB[mb], stB, gB], writes=[mhB[mb]])
                for g in range(2):
                    P.op("pe", [TR(psb[:, i * 128:(i + 1) * 128], mh[mb][:, (g * 8 + i) * 128:(g * 8 + i + 1) * 128], ident)
                                for i in range(8)], reads=[mhB[mb]], writes=[bankB[7]])
                    P.op("dve", CP(memT[:, g * 8:(g + 1) * 8, mb * 128:(mb + 1) * 128], psb.rearrange("p (a b) -> p a b", b=128)),
                         reads=[bankB[7]], writes=[memTB])
            wv = w_mkv[l].rearrange("(c p) n -> p c n", p=128)
            pr = Ring([0, 1, 2, 3, 4, 5])
            for ws in range(2):
                sl = ws
                P.dma("pool", sm[f"w{sl}"], wsl[sl], wv[:, :, ws * 512:(ws + 1) * 512], writes=[wB[sl]])
                if ws == 0:
                    for j in range(4):
                        bk = pr.next()
                        P.op("pe", [MM(bank[bk][:, 0:256], wsl[sl][:, c, j * 128:(j + 1) * 128], memT[:, c, :], c == 0, c == 15)
                                    for c in range(16)], reads=[wB[sl], memTB], writes=[bankB[bk]])
                        P.op("dve", CP(mkT[:, j, :], bank[bk][:, 0:256]), reads=[bankB[bk]], writes=[mkB])
                else:
                    for mb in range(2):
                        bk = pr.next()
                        P.op("pe", [MM(bank[bk], memT[:, c, mb * 128:(mb + 1) * 128], wsl[sl][:, c, :], c == 0, c == 15)
                                    for c in range(16)], reads=[wB[sl], memTB], writes=[bankB[bk]])
                        P.op("dve", CP(mv[:, mb, :], bank[bk]), reads=[bankB[bk]], writes=[mvB])
            for hm in range(2):
                for dc in range(2):
                    P.dma("sp", sm[f"q{dc}"], qT[dc], proj[MQ + 2 * hm + dc], writes=[qB[dc]])
                    P.dma("sp", sm[f"z{dc}"], mz[dc], proj[MZ + 2 * hm + dc], writes=[zB[dc]])
                for tc in range(8):
                    csl = slice(tc * 512, (tc + 1) * 512)
                    for mb in range(2):
                        bk = pr.next()
                        P.op("pe", [MM(bank[bk], mkT[:, 2 * hm + dc, mb * 128:(mb + 1) * 128], qT[dc][:, csl], dc == 0, dc == 1)
                                    for dc in range(2)], reads=[mkB, qB[0], qB[1]], writes=[bankB[bk]])
                        P.op("act", ACT(Eb[mb], bank[bk], AF.Exp), reads=[bankB[bk]], writes=[EB[mb]])
                    zk = pr.next()
                    P.op("pe", [MM(bank[zk], ones_bf, Eb[mb], mb == 0, mb == 1) for mb in range(2)],
                         reads=[EB[0], EB[1]], writes=[bankB[zk]])
                    P.op("dve", RCP(tm[0], bank[zk]), reads=[bankB[zk]], writes=[tB[0]])
                    for db in range(2):
                        ok = pr.next()
                        c0 = hm * 256 + db * 128
                        P.op("pe", [MM(bank[ok], mv[:, mb, c0:c0 + 128], Eb[mb], mb == 0, mb == 1) for mb in range(2)],
                             reads=[mvB, EB[0], EB[1]], writes=[bankB[ok]])
                        P.op("dve", TT(tm[1], bank[ok], tm[0], ALU.mult), reads=[bankB[ok], tB[0]], writes=[tB[1]])
                        P.op("dve", TT(ostg[db][:, csl], tm[1], mz[db][:, csl], ALU.mult), reads=[tB[1], zB[db]], writes=[oB[db]])
                for db in range(2):
                    P.dma("sp", sm[f"o{db}"], br_loc[(12 + 2 * hm + db) * 128:(13 + 2 * hm + db) * 128, :], ostg[db], reads=[oB[db]])
            P.barrier()
            gather_branch(3)

        def phase_C(l, xown_src, final):
            A.off = base_mark
            brc2 = [A.alloc([4, 8, 512], BF16) for _ in range(2)]
            sg = [A.alloc([4, 512], BF16) for _ in range(3)]
            mT = [A.alloc([8, 512], BF16) for _ in range(2)]
            wbr = A.alloc([4, 8, 1024], BF16)
            tm = [A.alloc([512], F32) for _ in range(4)]
            brB2 = [[Buf() for _ in range(4)] for _ in range(2)]
            sgB = [Buf() for _ in range(3)]
            mTB = [Buf() for _ in range(2)]
            wbB = [[Buf() for _ in range(2)] for _ in range(4)]
            tB = [Buf() for _ in range(4)]
            mglocB = [Buf() for _ in range(4)]
            sm = {k: P.dma_sem() for k in ["b0", "b1", "b2", "b3", "c0", "c1", "c2", "c3", "s0", "s1", "s2", "m0", "m1"]}
            wsm = [[P.dma_sem(sw=True) for _ in range(2)] for _ in range(4)]
            for dg in range(2):
                for n in range(4):
                    P.dma("pool", wsm[n][dg], wbr[:, n, :, dg * 512:(dg + 1) * 512],
                          w_br[l, n].rearrange("(w p) d -> p w d", p=128)[:, :, dg * 512:(dg + 1) * 512], writes=[wbB[n][dg]])
            sg_i = 0
            for tc in range(8):
                csl = slice(tc * 512, (tc + 1) * 512)
                m = mT[tc % 2]
                mB = mTB[tc % 2]
                brc = brc2[tc % 2]
                brB = brB2[tc % 2]
                for n in range(4):
                    P.dma("sp", sm[("b%d" if tc % 2 == 0 else "c%d") % n], brc[:, n, :, :],
                          br_all[n * 2:(n + 1) * 2].rearrange("k (j p) t -> p (k j) t", p=128)[:, :, csl],
                          reads=[brallB[2 * n], brallB[2 * n + 1]], writes=[brB[n]])
                for dg in range(2):
                    for jj in range(4):
                        dl = dg * 4 + jj
                        si = sg_i % 3
                        sg_i += 1
                        P.dma("sp", sm[f"s{si}"], sg[si], proj[GT + dl:GT + 32:8, :, csl].rearrange("n p t -> p n t"), writes=[sgB[si]])
                        bset = (0, 1, 2, 3) if dl % 2 == 0 else (4, 5, 6, 7)
                        for n in range(4):
                            P.op("pe", [MM(bank[bset[n]], wbr[:, n, wc, dl * 128:(dl + 1) * 128], brc[:, n, wc, :], wc == 0, wc == 7)
                                        for wc in range(8)], reads=[wbB[n][dg], brB[n]], writes=[bankB[bset[n]]])
                        for n in range(4):
                            P.op("dve", TT(tm[n], bank[bset[n]], sg[si][:, n, :], ALU.mult), reads=[bankB[bset[n]], sgB[si]], writes=[tB[n]])
                        P.op("pool", TT(tm[0], tm[0], tm[1], ALU.add), reads=[tB[0], tB[1]], writes=[tB[0]])
                        P.op("pool", TT(tm[2], tm[2], tm[3], ALU.add), reads=[tB[2], tB[3]], writes=[tB[2]])
                        P.op("pool", TT(m[:, dl, :], tm[0], tm[2], ALU.add), reads=[tB[0], tB[2]], writes=[mB])
                q = tc // 2
                P.dma("act", sm[f"m{tc % 2}"], mg_loc[q].rearrange("(d p) t -> p d t", p=128)[:, :, (tc % 2) * 512:(tc % 2 + 1) * 512],
                      m, reads=[mB], writes=[mglocB[q]])
                if tc % 2 == 1:
                    P.coll(mgS[q], "AllGather", PAIRS, mg_all[q], mg_loc[q], reads=[mglocB[q]], writes=[mgallB[q]])
            P.barrier()
            A.off = base_mark
            wo_sb = [A.alloc([16, 512], BF16) for _ in range(2)]
            mall = [A.alloc([16, 512], BF16) for _ in range(2)]
            xt = [A.alloc([1024], F32) for _ in range(4)]
            woB = Buf()
            maB = [Buf() for _ in range(2)]
            xB = [Buf() for _ in range(4)]
            xoB = [[Buf() for _ in range(4)] for _ in range(8)]
            sm = {k: P.dma_sem(sw=(k == "w")) for k in ["w", "a0", "a1", "x0", "x1", "x2", "x3"]}
            wo = w_out[l].rearrange("(c p) n -> p c n", p=128)
            for cg in range(2):
                P.dma("pool", sm["w"], wo_sb[cg], wo[:, :, cg * 512:(cg + 1) * 512], writes=[woB])
            yr = Ring([0, 1, 2, 3, 4, 5, 6, 7])
            for tc in range(8):
                q = tc // 2
                ma = mall[tc % 2]
                P.dma("sp", sm[f"a{tc % 2}"], ma, mg_all[q].rearrange("(c p) t -> p c t", p=128)[:, :, (tc % 2) * 512:(tc % 2 + 1) * 512],
                      reads=[mgallB[q]], writes=[maB[tc % 2]])
                for tb in range(4):
                    P.dma("sp", sm[f"x{tb}"], xt[tb], xown_src[tc][tb * 128:(tb + 1) * 128, :], writes=[xB[tb]])
                    for cg in range(2):
                        bk = yr.next()
                        P.op("pe", [MM(bank[bk], ma[:, c, tb * 128:(tb + 1) * 128], wo_sb[cg][:, c, :], c == 0, c == 15) for c in range(16)],
                             reads=[maB[tc % 2], woB], writes=[bankB[bk]])
                        P.op("dve", TT(xt[tb][:, cg * 512:(cg + 1) * 512], xt[tb][:, cg * 512:(cg + 1) * 512], bank[bk], ALU.add),
                             reads=[bankB[bk], xB[tb]], writes=[xB[tb]])
                    P.dma("act", sm[f"x{tb}"], xo[tc][tb * 128:(tb + 1) * 128, :], xt[tb], reads=[xB[tb]], writes=[xoB[tc][tb]])
                P.coll(xfS[tc], "AllGather", PAIRS, xf[tc], xo[tc], reads=xoB[tc], writes=[xfB[tc]])
            P.barrier()
            if final:
                A.off = base_mark
                fg = A.alloc([1024], F32)
                fl = [A.alloc([D], F32) for _ in range(2)]
                ow = [A.alloc([1024], F32) for _ in range(2)]
                junk = A.alloc([D], BF16)
                stat = A.alloc([8], F32)
                fgB, jB = Buf(), Buf()
                flB = [Buf() for _ in range(2)]
                owB = [Buf() for _ in range(2)]
                stB = [Buf() for _ in range(2)]
                sm = {k: P.dma_sem() for k in ["g", "f0", "f1", "o0", "o1"]}
                ssw = [P.dma_sem(sw=True) for _ in range(2)]
                P.dma("sp", sm["g"], fg, fing_bc, writes=[fgB])
                for gtb in range(32):
                    tc, tq = gtb // 4, gtb % 4
                    k = gtb % 2
                    for r in range(2):
                        P.dma("sp", sm[f"f{k}"], fl[k][:, r * 1024:(r + 1) * 1024], xf[tc][r * 512 + tq * 128:r * 512 + (tq + 1) * 128, :],
                              reads=[xfB[tc]], writes=[flB[k]])
                    P.dma("sp", sm[f"o{k}"], ow[k], xo[tc][tq * 128:(tq + 1) * 128, :], writes=[owB[k]])
                    c = k * 3
                    P.op("act", ACT(junk, fl[k], AF.Square, accum_out=stat[:, c:c + 1]), reads=[flB[k]], writes=[jB, stB[k]])
                    P.op("act", ACT(stat[:, c + 1:c + 2], stat[:, c:c + 1], AF.Sqrt, scale=1.0 / D, bias=EPS), reads=[stB[k]], writes=[stB[k]])
                    P.op("dve", RCP(stat[:, c + 2:c + 3], stat[:, c + 1:c + 2]), reads=[stB[k]], writes=[stB[k]])
                    P.op("dve", STT(ow[k], ow[k], stat[:, c + 2:c + 3], fg, ALU.mult, ALU.mult), reads=[owB[k], stB[k], fgB], writes=[owB[k]])
                    P.dma("pool", ssw[k], y_out[tc][tq * 128:(tq + 1) * 128, :], ow[k], reads=[owB[k]])
                P.barrier()

        for l in range(n_layers):
            final = (l == DEPTH - 1)
            if "A" in phases:
                phase_A(l, x_full if l == 0 else xf)
            if "1" in phases:
                phase_DA(l)
            if "2" in phases:
                phase_SB(l)
            if "3" in phases:
                phase_pool(l)
            if "4" in phases:
                phase_mem(l)
            if "C" in phases:
                phase_C(l, x_own if l == 0 else xo, final)
        P.barrier(final=True)
        counts = P.emit(nc, st)
    return nc, counts


def host_constants():
    bf = ml_dtypes.bfloat16
    k = np.arange(128)
    c = {}
    c["c_ident"] = np.eye(128, dtype=np.float32).astype(bf)
    c["c_J"] = np.eye(128, dtype=np.float32)[::-1].copy().astype(bf)
    c["c_trineg"] = (-(k[:, None] >= k[None, :]).astype(np.float32)).astype(bf)
    c["c_ones"] = np.ones((128, 128), np.float32).astype(bf)
    c["c_zeros"] = np.zeros((128, 128), np.float32).astype(bf)
    mt = (k[:, None] < k[None, :]).astype(np.float32)
    c["c_mtri"] = mt.astype(bf)
    c["c_negm"] = (NEG * (1.0 - mt)).astype(np.float32)
    c["c_onesf"] = np.full((128, 128), 1.0 / 128.0, np.float32)
    c["c_ones1f"] = np.ones((128, 128), np.float32)
    n = np.arange(GL) - 511
    oh = np.zeros((33, GL), np.float32)
    bk = t5_bucket_np(n)
    for i in range(GL):
        if n[i] >= 0:
            oh[bk[i], i] = 1.0
        else:
            oh[32, i] = 1.0
    c["c_oh"] = oh
    invc = np.zeros((128, 4, 16), np.float32)
    t = np.arange(16)
    for g, w in enumerate((2, 4, 8, 16)):
        invc[:, g, :] = 1.0 / np.minimum(t + 1, w).astype(np.float32)
    c["c_invc"] = invc
    return c


def _in_cols(r):
    cols = []
    for i in range(8):
        cols.append(i * 1024 + r * 512 + np.arange(512))
    cols.append(8 * 1024 + np.arange(1024))
    for g in range(4):
        cols.append(9 * 1024 + (2 * g + r) * 128 + np.arange(128))
    cols.append(10 * 1024 + r * 512 + np.arange(512))
    cols.append(11 * 1024 + r * 512 + np.arange(512))
    for n in range(4):
        cols.append(12288 + n * 2048 + r * 1024 + np.arange(1024))
    return np.concatenate(cols)


def _branch_rows(r_unused):
    rows = []
    for n in range(4):
        blk = []
        for k in range(2):
            for rr in range(2):
                for i in range(2):
                    loc = 2 * k + i
                    blk.append(2 * loc + rr if n == 2 else 4 * rr + loc)
        rows.append(np.concatenate([b * 128 + np.arange(128) for b in blk]))
    return rows


def host_shared(inputs, r):
    f = lambda a: np.asarray(a, dtype=np.float32)
    m = {}
    m["w_in"] = np.ascontiguousarray(f(inputs["w_in"])[:, :, _in_cols(r)])
    wkv = f(inputs["w_mem_kv"])
    m["w_mem_kv"] = np.ascontiguousarray(np.concatenate([wkv[:, :, r * 512:(r + 1) * 512], wkv[:, :, 1024 + r * 512:1024 + (r + 1) * 512]], axis=2))
    wb = f(inputs["w_branch"])
    rows = _branch_rows(r)
    m["w_branch"] = np.ascontiguousarray(np.stack([wb[:, n][:, rows[n]][:, :, r * 1024:(r + 1) * 1024] for n in range(4)], axis=1))
    m["w_out"] = np.ascontiguousarray(f(inputs["w_out"])[:, :, r * 1024:(r + 1) * 1024])
    m["w_pool"] = np.ascontiguousarray(f(inputs["w_pool"])[:, :, :, r * 128:(r + 1) * 128])
    bc = lambda v: np.ascontiguousarray(np.broadcast_to(f(v)[..., None, :], v.shape[:-1] + (128, v.shape[-1])))
    m["norm_g_bc"] = bc(inputs["norm_g"])
    m["mem_norm_g_bc"] = bc(inputs["mem_norm_g"])
    m["final_g_bc"] = bc(f(inputs["final_g"])[r * 1024:(r + 1) * 1024])
    gbt = f(inputs["gate_b"]).reshape(DEPTH, 4, 2, 8, 128)[:, :, r]
    m["gate_b_t"] = np.ascontiguousarray(gbt.transpose(0, 3, 1, 2).reshape(DEPTH, 128, 32))
    m["da_g_t"] = np.ascontiguousarray(f(inputs["da_norm_g"]).reshape(DEPTH, 8, 128)[:, r * 4:(r + 1) * 4].transpose(0, 2, 1))
    m["pool_scale_t"] = np.ascontiguousarray(f(inputs["pool_scale"]).reshape(DEPTH, 4, 2, 128)[:, :, r].transpose(0, 2, 1))
    lam = np.stack([f(inputs[k]) for k in ("lam_q1", "lam_k1", "lam_q2", "lam_k2")], axis=1)
    m["lam_bc"] = np.ascontiguousarray(np.broadcast_to(lam[:, None], (DEPTH, 128, 4, 64)))
    rb = f(inputs["rel_bias"])[:, r * 4:(r + 1) * 4]
    m["rel_bias"] = np.ascontiguousarray(rb)
    m["rb31_bc"] = np.ascontiguousarray(np.broadcast_to(rb[31][None], (128, 4)))
    return m


def host_acts(inputs, b, r):
    f = lambda a: np.asarray(a, dtype=np.float32)
    x = f(inputs["x"][b])
    m = {}
    m["x_full"] = np.ascontiguousarray(x.reshape(8, 512, 2, 1024).transpose(0, 2, 1, 3).reshape(8, 1024, 1024))
    m["x_own"] = np.ascontiguousarray(x[:, r * 1024:(r + 1) * 1024].reshape(8, 512, 1024))
    m["mem"] = np.ascontiguousarray(f(inputs["mem"][b]))
    return m


_NC_CACHE = {}


def kernel(**inputs):
    if "nc" not in _NC_CACHE:
        _NC_CACHE["nc"] = build_program()[0]
    nc = _NC_CACHE["nc"]
    consts = host_constants()
    shared = [host_shared(inputs, r) for r in range(2)]
    in_maps = []
    for core in range(N_CORES):
        b, r = core // 2, core % 2
        m = dict(shared[r])
        m.update(host_acts(inputs, b, r))
        m.update(consts)
        in_maps.append(m)
    res = run_bass_kernel_spmd(nc, in_maps, core_ids=list(range(N_CORES)))
    out = np.empty((4, S, D), np.float32)
    for core in range(N_CORES):
        b, r = core // 2, core % 2
        out[b][:, r * 1024:(r + 1) * 1024] = np.asarray(res.results[core]["y"], dtype=np.float32).reshape(S, 1024)
    return out
```
